# Optimizing a Trainium2 kernel written in Bass

```python
import math
import jax, jax.numpy as jnp
from jax import lax
import numpy as np

D_MODEL = 1024
BATCH = 8
SEQ = 8192
DEPTH = 4
DEC_BATCH = 16
DEC_SEQ = 2048
PAST_LEN = 128

N_MEM = 256
Q_BLOCK = 128
EPS = 1e-6
N_MIXERS = 2
N_LAYERS_A = (DEPTH + 1) // 2
N_LAYERS_B = DEPTH // 2
MLA_HEADS = 8
MLA_Q_LORA = 384
MLA_KV_LORA = 256
MLA_NOPE = 64
MLA_ROPE = 32
MLA_QK = MLA_NOPE + MLA_ROPE
MLA_V = 64
ROPE_THETA = 10000.0
DIFF_HEADS = 8
DIFF_HD = 64
XA_HEADS = 4
XA_HD = D_MODEL // XA_HEADS
D_FF = 2816
CONV_W = 3

kernel_name = 'hybrid_mla_diffattn_convglu_encoder'


def _rmsnorm(x, g):
    x32 = x.astype(jnp.float32)
    y = x32 * lax.rsqrt(jnp.mean(x32 * x32, axis=-1, keepdims=True) + EPS)
    return (y * g.astype(jnp.float32)).astype(x.dtype)


def _sweep_query_blocks(fn, q):
    b, s = q.shape[:2]
    nb = s // Q_BLOCK
    qb = jnp.moveaxis(q.reshape((b, nb, Q_BLOCK) + q.shape[2:]), 1, 0)
    out = lax.map(lambda a: fn(a[0], a[1]), (qb, jnp.arange(nb)))
    out = jnp.moveaxis(out, 0, 1)
    return out.reshape((b, s) + out.shape[3:])


def _rope_tables(seq):
    inv = ROPE_THETA ** (-jnp.arange(0, MLA_ROPE, 2, dtype=jnp.float32) / MLA_ROPE)
    ang = jnp.arange(seq, dtype=jnp.float32)[:, None] * inv[None, :]
    ang = jnp.concatenate([ang, ang], axis=-1)
    return jnp.cos(ang), jnp.sin(ang)


def _apply_rope(x, cos, sin):
    x32 = x.astype(jnp.float32)
    half = MLA_ROPE // 2
    rot = jnp.concatenate([-x32[..., half:], x32[..., :half]], axis=-1)
    return (x32 * cos[None, :, None, :] + rot * sin[None, :, None, :]).astype(x.dtype)


def _alibi_slopes(n):
    return 2.0 ** (-8.0 * jnp.arange(1, n + 1, dtype=jnp.float32) / n)


def _mla(x, norm_g, w_down, q_lat_g, kv_lat_g, w_uq, w_ukv, q_g, k_g, w_o):
    b, s, _ = x.shape
    h = _rmsnorm(x, norm_g)
    down = h @ w_down
    c_q, c_kv, k_rope = jnp.split(down, [MLA_Q_LORA, MLA_Q_LORA + MLA_KV_LORA], axis=-1)
    c_q = _rmsnorm(c_q, q_lat_g)
    c_kv = _rmsnorm(c_kv, kv_lat_g)
    q = (c_q @ w_uq).reshape(b, s, MLA_HEADS, MLA_QK)
    kv = (c_kv @ w_ukv).reshape(b, s, MLA_HEADS, MLA_NOPE + MLA_V)
    k_nope, v = jnp.split(kv, [MLA_NOPE], axis=-1)
    k_rope = jnp.broadcast_to(k_rope[:, :, None, :], (b, s, MLA_HEADS, MLA_ROPE))
    k = jnp.concatenate([k_nope, k_rope], axis=-1)
    q = _rmsnorm(q, q_g)
    k = _rmsnorm(k, k_g)
    cos, sin = _rope_tables(s)
    q = jnp.concatenate([q[..., :MLA_NOPE], _apply_rope(q[..., MLA_NOPE:], cos, sin)], axis=-1)
    k = jnp.concatenate([k[..., :MLA_NOPE], _apply_rope(k[..., MLA_NOPE:], cos, sin)], axis=-1)
    scale = MLA_QK ** -0.5

    def block(qb, bi):
        sc = jnp.einsum('bqhd,bkhd->bhqk', qb, k).astype(jnp.float32) * scale
        p = jax.nn.softmax(sc, axis=-1).astype(v.dtype)
        return jnp.einsum('bhqk,bkhe->bqhe', p, v)

    o = _sweep_query_blocks(block, q)
    return o.reshape(b, s, MLA_HEADS * MLA_V) @ w_o


def _diff_attn(x, layer_idx, norm_g, w_qkv, q_g, k_g, lam_p, sub_g, w_o):
    b, s, _ = x.shape
    lambda_init = 0.8 - 0.6 * math.exp(-0.3 * layer_idx)
    h = _rmsnorm(x, norm_g)
    q, k, v = jnp.split(h @ w_qkv, 3, axis=-1)
    q = _rmsnorm(q.reshape(b, s, DIFF_HEADS, 2, DIFF_HD), q_g)
    k = _rmsnorm(k.reshape(b, s, DIFF_HEADS, 2, DIFF_HD), k_g)
    v = v.reshape(b, s, DIFF_HEADS, 2 * DIFF_HD)
    lp = lam_p.astype(jnp.float32)
    lam = jnp.exp(jnp.sum(lp[0] * lp[1])) - jnp.exp(jnp.sum(lp[2] * lp[3])) + lambda_init
    slopes = _alibi_slopes(DIFF_HEADS)
    tk = jnp.arange(s)
    scale = DIFF_HD ** -0.5

    def block(qb, bi):
        tq = bi * Q_BLOCK + jnp.arange(Q_BLOCK)
        dist = jnp.abs(tq[:, None] - tk[None, :]).astype(jnp.float32)
        bias = -slopes[:, None, None] * dist[None]
        sc = jnp.einsum('bqhcd,bkhcd->bchqk', qb, k).astype(jnp.float32) * scale + bias
        p = jax.nn.softmax(sc, axis=-1)
        a = (p[:, 0] - lam * p[:, 1]).astype(v.dtype)
        return jnp.einsum('bhqk,bkhe->bqhe', a, v)

    o = _sweep_query_blocks(block, q)
    o = _rmsnorm(o, sub_g) * (1.0 - lambda_init)
    return o.reshape(b, s, DIFF_HEADS * 2 * DIFF_HD) @ w_o


def _mem_xattn(x, mem, norm_g, mem_g, w_q, w_kv, q_g, k_g, w_o):
    b, s, _ = x.shape
    m = mem.shape[1]
    q = (_rmsnorm(x, norm_g) @ w_q).reshape(b, s, XA_HEADS, XA_HD)
    k, v = jnp.split(_rmsnorm(mem, mem_g) @ w_kv, 2, axis=-1)
    k = k.reshape(b, m, XA_HEADS, XA_HD)
    v = v.reshape(b, m, XA_HEADS, XA_HD)
    q = _rmsnorm(q, q_g)
    k = _rmsnorm(k, k_g)
    sc = jnp.einsum('bqhd,bmhd->bhqm', q, k).astype(jnp.float32) * (XA_HD ** -0.5)
    p = jax.nn.softmax(sc, axis=-1).astype(v.dtype)
    o = jnp.einsum('bhqm,bmhd->bqhd', p, v).reshape(b, s, XA_HEADS * XA_HD)
    return o @ w_o


def _conv_glu(x, norm_g, w_gu, conv_w, conv_b, w_down):
    s = x.shape[1]
    h = _rmsnorm(x, norm_g)
    g, u = jnp.split(h @ w_gu, 2, axis=-1)
    pad = CONV_W // 2
    gp = jnp.pad(g, ((0, 0), (pad, pad), (0, 0)))
    g = sum(gp[:, j:j + s] * conv_w[j] for j in range(CONV_W)) + conv_b
    return (jax.nn.silu(g) * u) @ w_down


def _trunk(x, mem, mla_p, diff_p, xa_p, ffn_p):
    for i in range(DEPTH):
        j = i // N_MIXERS
        if i % N_MIXERS == 0:
            x = x + _mla(x, *[p[j] for p in mla_p])
        else:
            x = x + _diff_attn(x, i, *[p[j] for p in diff_p])
        x = x + _mem_xattn(x, mem, *[p[i] for p in xa_p])
        x = x + _conv_glu(x, *[p[i] for p in ffn_p])
    return x


def setup_inputs(seed: int = 0) -> dict:
    key = jax.random.key(seed)
    keys = iter(jax.random.split(key, 32))
    f32 = jnp.float32

    def w(shape, fan_in):
        return jax.random.normal(next(keys), shape, f32) * (fan_in ** -0.5)

    def g(shape):
        return 1.0 + 0.02 * jax.random.normal(next(keys), shape, f32)

    D = D_MODEL
    A, B = N_LAYERS_A, N_LAYERS_B
    return {
        'x_prompt': jax.random.normal(next(keys), (BATCH, SEQ, D), f32),
        'x_sample': jax.random.normal(next(keys), (DEC_BATCH, DEC_SEQ, D), f32),
        'mem_prompt': jax.random.normal(next(keys), (BATCH, N_MEM, D), f32),
        'mem_sample': jax.random.normal(next(keys), (DEC_BATCH, N_MEM, D), f32),
        'mla_norm': g((A, D)),
        'mla_w_down': w((A, D, MLA_Q_LORA + MLA_KV_LORA + MLA_ROPE), D),
        'mla_q_lat_norm': g((A, MLA_Q_LORA)),
        'mla_kv_lat_norm': g((A, MLA_KV_LORA)),
        'mla_w_uq': w((A, MLA_Q_LORA, MLA_HEADS * MLA_QK), MLA_Q_LORA),
        'mla_w_ukv': w((A, MLA_KV_LORA, MLA_HEADS * (MLA_NOPE + MLA_V)), MLA_KV_LORA),
        'mla_q_norm': g((A, MLA_QK)),
        'mla_k_norm': g((A, MLA_QK)),
        'mla_w_o': w((A, MLA_HEADS * MLA_V, D), MLA_HEADS * MLA_V),
        'diff_norm': g((B, D)),
        'diff_w_qkv': w((B, D, 3 * DIFF_HEADS * 2 * DIFF_HD), D),
        'diff_q_norm': g((B, 2, DIFF_HD)),
        'diff_k_norm': g((B, 2, DIFF_HD)),
        'diff_lambda': 0.1 * jax.random.normal(next(keys), (B, 4, DIFF_HD), f32),
        'diff_sub_norm': g((B, 2 * DIFF_HD)),
        'diff_w_o': w((B, DIFF_HEADS * 2 * DIFF_HD, D), DIFF_HEADS * 2 * DIFF_HD),
        'xa_norm': g((DEPTH, D)),
        'xa_mem_norm': g((DEPTH, D)),
        'xa_w_q': w((DEPTH, D, XA_HEADS * XA_HD), D),
        'xa_w_kv': w((DEPTH, D, 2 * XA_HEADS * XA_HD), D),
        'xa_q_norm': g((DEPTH, XA_HD)),
        'xa_k_norm': g((DEPTH, XA_HD)),
        'xa_w_o': w((DEPTH, XA_HEADS * XA_HD, D), XA_HEADS * XA_HD),
        'ffn_norm': g((DEPTH, D)),
        'ffn_w_gu': w((DEPTH, D, 2 * D_FF), D),
        'ffn_conv_w': w((DEPTH, CONV_W, D_FF), CONV_W),
        'ffn_conv_b': 0.02 * jax.random.normal(next(keys), (DEPTH, D_FF), f32),
        'ffn_w_down': w((DEPTH, D_FF, D), D_FF),
    }


def reference(x_prompt, x_sample, mem_prompt, mem_sample,
              mla_norm, mla_w_down, mla_q_lat_norm, mla_kv_lat_norm, mla_w_uq, mla_w_ukv,
              mla_q_norm, mla_k_norm, mla_w_o,
              diff_norm, diff_w_qkv, diff_q_norm, diff_k_norm, diff_lambda, diff_sub_norm, diff_w_o,
              xa_norm, xa_mem_norm, xa_w_q, xa_w_kv, xa_q_norm, xa_k_norm, xa_w_o,
              ffn_norm, ffn_w_gu, ffn_conv_w, ffn_conv_b, ffn_w_down):
    mla_p = (mla_norm, mla_w_down, mla_q_lat_norm, mla_kv_lat_norm, mla_w_uq, mla_w_ukv,
             mla_q_norm, mla_k_norm, mla_w_o)
    diff_p = (diff_norm, diff_w_qkv, diff_q_norm, diff_k_norm, diff_lambda, diff_sub_norm, diff_w_o)
    xa_p = (xa_norm, xa_mem_norm, xa_w_q, xa_w_kv, xa_q_norm, xa_k_norm, xa_w_o)
    ffn_p = (ffn_norm, ffn_w_gu, ffn_conv_w, ffn_conv_b, ffn_w_down)
    y_prompt = _trunk(x_prompt, mem_prompt, mla_p, diff_p, xa_p, ffn_p)
    y_sample = _trunk(x_sample, mem_sample, mla_p, diff_p, xa_p, ffn_p)
    return (y_prompt, y_sample)
```

```python
import math
from contextlib import ExitStack
import numpy as np
import concourse.bass as bass
import concourse.mybir as mybir
from concourse.bass_utils import run_bass_kernel_spmd

F32 = mybir.dt.float32
BF16 = mybir.dt.bfloat16
AF = mybir.ActivationFunctionType
ALU = mybir.AluOpType

D = 1024
NMEM = 256
EPS = 1e-6
DFF = 2816
NFC = DFF // 128
ROPE_THETA = 10000.0


class MK:
    def __init__(self, nc, es):
        self.nc = nc
        self.es = es
        self.engs = {"pe": nc.tensor, "act": nc.scalar, "dve": nc.vector, "pool": nc.gpsimd, "sp": nc.sync}
        self.sem = {}
        self.cnt = {}
        for e in self.engs:
            self.sem[e] = es.enter_context(nc.semaphore("s_" + e))
            self.cnt[e] = 0
        self.known = {e: {} for e in self.engs}
        self.lastw = {}
        self.readers = {}
        self.nops = 0

    def _chan(self, ch):
        if ch not in self.sem:
            self.sem[ch] = self.es.enter_context(self.nc.semaphore("d_" + ch))
            self.cnt[ch] = 0
        return self.sem[ch]

    def _deps(self, e, r, w, extra=()):
        deps = {}
        def add(t):
            s, c, de = t
            if de == "pe" and e == "pe":
                return
            if deps.get(s, 0) < c:
                deps[s] = c
        for k in list(r) + list(w):
            if k in self.lastw:
                add(self.lastw[k])
        for k in w:
            for s, (c, de) in self.readers.get(k, {}).items():
                add((s, c, de))
        for t in extra:
            add(t)
        eng = self.engs[e]
        kn = self.known[e]
        for s, c in deps.items():
            if kn.get(s, 0) >= c:
                continue
            eng.wait_ge(self.sem[s], c)
            kn[s] = c
            self.nops += 1

    def _record(self, me, r, w):
        s, c, de = me
        for k in w:
            self.lastw[k] = me
            self.readers[k] = {}
        for k in r:
            self.readers.setdefault(k, {})[s] = (c, de)

    def op(self, e, fn, r=(), w=()):
        self._deps(e, r, w)
        ins = fn(self.engs[e])
        self.cnt[e] += 1
        ins.then_inc(self.sem[e], 1)
        self.nops += 1
        self._record((e, self.cnt[e], e), r, w)

    def dma(self, ch, fns, r=(), w=()):
        sem = self._chan(ch)
        extra = [(ch, self.cnt[ch], "dma")] if self.cnt[ch] else []
        self._deps("sp", r, w, extra)
        if not isinstance(fns, (list, tuple)):
            fns = [fns]
        for fn in fns:
            ins = fn(self.nc.sync)
            ins.then_inc(sem, 16)
            self.cnt[ch] += 16
            self.nops += 1
        self._record((ch, self.cnt[ch], "dma"), r, w)

    def barrier(self):
        for e in self.engs:
            eng = self.engs[e]
            kn = self.known[e]
            for s, c in self.cnt.items():
                if c and kn.get(s, 0) < c and not (s == e == "pe"):
                    eng.wait_ge(self.sem[s], c)
                    kn[s] = c
                    self.nops += 1

    def finish(self):
        kn = self.known["sp"]
        for s, c in self.cnt.items():
            if c and kn.get(s, 0) < c:
                self.nc.sync.wait_ge(self.sem[s], c)
                kn[s] = c


def _gpack_layout(depth):
    off = {}
    n = 0
    nA = (depth + 1) // 2
    nB = depth // 2
    def add(name, w):
        nonlocal n
        off[name] = n
        n += w
    for j in range(nA):
        add(("mla_norm", j), 8); add(("qlat", j), 3); add(("kvlat", j), 2)
        add(("g1q", j), 1); add(("g2q", j), 1); add(("g1k", j), 1); add(("g2k", j), 1)
    for j in range(nB):
        add(("diff_norm", j), 8); add(("dq", j), 1); add(("dk", j), 1); add(("dsub", j), 1)
    for i in range(depth):
        add(("xa_norm", i), 8); add(("xa_mem", i), 8); add(("xq", i), 2); add(("xk", i), 2)
        add(("ffn_norm", i), 8); add(("cw", i), 3 * NFC); add(("cb", i), NFC)
    return off, n


def _cols(v, nchunk):
    return np.ascontiguousarray(np.asarray(v, np.float32).reshape(nchunk, 128).T)


def _host_prep(inp, depth):
    nA = (depth + 1) // 2
    nB = depth // 2
    off, ncol = _gpack_layout(depth)
    gp = np.zeros((128, ncol), np.float32)
    out = {}
    pr = (np.arange(32) + 16) % 32
    if nA:
        wd = inp["mla_w_down"][:nA]
        out["mla_wd"] = np.ascontiguousarray(np.concatenate([wd[:, :, :640], wd[:, :, 640:672], wd[:, :, 640 + pr]], axis=2))
        wuq = inp["mla_w_uq"][:nA].reshape(nA, 384, 8, 96)
        q1 = np.zeros((nA, 384, 8, 128), np.float32)
        q1[..., 0:32] = wuq[..., 64:96]
        q1[..., 64:128] = wuq[..., 0:64]
        out["mla_wq1"] = q1.reshape(nA, 384, 1024)
        out["mla_wq2"] = np.ascontiguousarray(wuq[..., 64 + pr]).reshape(nA, 384, 256)
        wkv = inp["mla_w_ukv"][:nA].reshape(nA, 256, 8, 128)
        kn = np.zeros((nA, 256, 8, 128), np.float32)
        kn[..., 64:128] = wkv[..., 0:64]
        out["mla_wkn"] = kn.reshape(nA, 256, 1024)
        out["mla_wv"] = np.ascontiguousarray(wkv[..., 64:128]).reshape(nA, 256, 512)
        out["mla_wo"] = np.ascontiguousarray(inp["mla_w_o"][:nA])
    for j in range(nA):
        gp[:, off[("mla_norm", j)]:][:, :8] = _cols(inp["mla_norm"][j], 8)
        gp[:, off[("qlat", j)]:][:, :3] = _cols(inp["mla_q_lat_norm"][j], 3)
        gp[:, off[("kvlat", j)]:][:, :2] = _cols(inp["mla_kv_lat_norm"][j], 2)
        for nm, g in (("q", inp["mla_q_norm"][j]), ("k", inp["mla_k_norm"][j])):
            c1 = off[("g1" + nm, j)]
            gp[0:32, c1] = g[64:96]
            gp[64:128, c1] = g[0:64]
            gp[0:32, off[("g2" + nm, j)]] = g[64 + pr]
    if nB:
        out["diff_wqkv"] = np.ascontiguousarray(inp["diff_w_qkv"][:nB])
        out["diff_wo"] = np.ascontiguousarray(inp["diff_w_o"][:nB])
        out["diff_lam"] = np.ascontiguousarray(inp["diff_lambda"][:nB].reshape(nB, 1, 256))
    for j in range(nB):
        gp[:, off[("diff_norm", j)]:][:, :8] = _cols(inp["diff_norm"][j], 8)
        gp[:, off[("dq", j)]] = inp["diff_q_norm"][j].reshape(128)
        gp[:, off[("dk", j)]] = inp["diff_k_norm"][j].reshape(128)
        gp[:, off[("dsub", j)]] = inp["diff_sub_norm"][j]
    for i in range(depth):
        gp[:, off[("xa_norm", i)]:][:, :8] = _cols(inp["xa_norm"][i], 8)
        gp[:, off[("xa_mem", i)]:][:, :8] = _cols(inp["xa_mem_norm"][i], 8)
        gp[:, off[("xq", i)]:][:, :2] = _cols(inp["xa_q_norm"][i], 2)
        gp[:, off[("xk", i)]:][:, :2] = _cols(inp["xa_k_norm"][i], 2)
        gp[:, off[("ffn_norm", i)]:][:, :8] = _cols(inp["ffn_norm"][i], 8)
        cw = inp["ffn_conv_w"][i]
        for t in range(3):
            gp[:, off[("cw", i)] + t * NFC:][:, :NFC] = _cols(cw[t], NFC)
        gp[:, off[("cb", i)]:][:, :NFC] = _cols(inp["ffn_conv_b"][i], NFC)
    out["xa_wq"] = np.ascontiguousarray(inp["xa_w_q"][:depth])
    out["xa_wkv"] = np.ascontiguousarray(inp["xa_w_kv"][:depth])
    out["xa_wo"] = np.ascontiguousarray(inp["xa_w_o"][:depth])
    out["ffn_wgu"] = np.ascontiguousarray(inp["ffn_w_gu"][:depth])
    out["ffn_wdn"] = np.ascontiguousarray(inp["ffn_w_down"][:depth])
    out["gpack"] = gp
    return out


def _const_tables(smax):
    c = {}
    c["ident"] = np.eye(128, dtype=np.float32)
    inv = ROPE_THETA ** (-np.arange(0, 32, 2, dtype=np.float32) / 32.0)
    ang = np.arange(smax, dtype=np.float32)[:, None] * inv[None, :]
    ang = np.concatenate([ang, ang], axis=-1).astype(np.float32)
    cos = np.cos(ang).T
    sin = np.sin(ang).T.copy()
    sin[0:16] *= -1.0
    c["ropecos"] = np.ascontiguousarray(cos, dtype=np.float32)
    c["ropessin"] = np.ascontiguousarray(sin, dtype=np.float32)
    import ml_dtypes
    pos = np.arange(smax)
    qa = np.zeros((8, 512), np.float32)
    q = np.arange(512)
    qa[0] = q % 128; qa[1] = 128 * (q // 128); qa[2] = 1; qa[3] = 1
    qa[4:8] = -qa[0:4]
    c["alibi_q"] = qa.astype(ml_dtypes.bfloat16)
    ka = np.zeros((8, 4, smax), np.float32)
    for h in range(8):
        m = 2.0 ** (-(h + 1)) * 8.0
        ka[h, 0] = -m; ka[h, 1] = -m
        ka[h, 2] = m * (pos % 128); ka[h, 3] = m * 128 * ((pos // 128) % 4)
    c["alibi_k"] = ka.astype(ml_dtypes.bfloat16)
    kk = np.arange(512)
    dist = np.abs(q[None, :] - kk[:, None]).astype(np.float32)
    c["alibi_diag"] = np.ascontiguousarray(dist.reshape(4, 128, 512).transpose(1, 0, 2))
    return c


def build_program(SP, SS, NS, depth):
    nc = bass.Bass("TRN2", target_bir_lowering=False)
    nA = (depth + 1) // 2
    nB = depth // 2
    goff, gcols = _gpack_layout(depth)
    seqs = []
    if SP:
        seqs.append(("p", SP))
    for i in range(NS):
        seqs.append(("s%d" % i, SS))
    smax = max(s for _, s in seqs)

    def din(name, shape, dt=F32):
        return nc.dram_tensor(name, list(shape), dt, kind="ExternalInput").ap()

    def dscr(name, shape, dt):
        return nc.dram_tensor(name, list(shape), dt).ap()

    xin, memin, yout = {}, {}, {}
    for nm, S in seqs:
        xin[nm] = din("x_" + nm, [S, D])
        memin[nm] = din("mem_" + nm, [NMEM, D])
        yout[nm] = nc.dram_tensor("y_" + nm, [S, D], F32, kind="ExternalOutput").ap()
    W = {}
    if nA:
        W["mla_wd"] = din("mla_wd", [nA, 1024, 704]); W["mla_wq1"] = din("mla_wq1", [nA, 384, 1024])
        W["mla_wq2"] = din("mla_wq2", [nA, 384, 256]); W["mla_wkn"] = din("mla_wkn", [nA, 256, 1024])
        W["mla_wv"] = din("mla_wv", [nA, 256, 512]); W["mla_wo"] = din("mla_wo", [nA, 512, 1024])
    if nB:
        W["diff_wqkv"] = din("diff_wqkv", [nB, 1024, 3072]); W["diff_wo"] = din("diff_wo", [nB, 1024, 1024])
        W["diff_lam"] = din("diff_lam", [nB, 1, 256])
    W["xa_wq"] = din("xa_wq", [depth, 1024, 1024]); W["xa_wkv"] = din("xa_wkv", [depth, 1024, 2048])
    W["xa_wo"] = din("xa_wo", [depth, 1024, 1024])
    W["ffn_wgu"] = din("ffn_wgu", [depth, 1024, 2 * DFF]); W["ffn_wdn"] = din("ffn_wdn", [depth, DFF, 1024])
    gpack_d = din("gpack", [128, gcols])
    ident_d = din("ident", [128, 128])
    cos_d = din("ropecos", [32, smax]); ssin_d = din("ropessin", [32, smax])
    alq_d = din("alibi_q", [8, 512], BF16); alk_d = din("alibi_k", [8, 4, smax], BF16)
    ald_d = din("alibi_diag", [128, 4, 512])

    X0, X1, QT, KT, VD, OT, MEMK, MEMV = {}, {}, {}, {}, {}, {}, {}, {}
    for nm, S in seqs:
        X0[nm] = dscr("X0_" + nm, [128, 8, S], F32)
        X1[nm] = dscr("X1_" + nm, [128, 8, S], F32)
        QT[nm] = dscr("QT_" + nm, [8, 128, S], BF16)
        KT[nm] = dscr("KT_" + nm, [8, 128, S], BF16)
        VD[nm] = dscr("VD_" + nm, [8, 128, S // 128, 130], BF16)
        OT[nm] = dscr("OT_" + nm, [8, 128, S], BF16)
        MEMK[nm] = dscr("MEMN_" + nm, [128, 8, NMEM], BF16)
    WDB = dscr("WDB", [2, 8, 128, NFC // 2, 128], BF16)

    es = ExitStack()
    with es:
        mk = MK(nc, es)
        op, dma = mk.op, mk.dma

        sbn = [0]

        def sb(name, shape, dt=F32, stack=es):
            sbn[0] += 1
            return stack.enter_context(nc.sbuf_tensor("sb%d_%s" % (sbn[0], name), list(shape), dt))

        ps = es.enter_context(nc.psum_tensor("ps", [128, 8, 512], F32))

        def PS(b):
            return ("ps", b)

        gv = sb("gv", [128, gcols])
        ident = sb("ident", [128, 128])
        ones_bf = sb("ones_bf", [128, 128], BF16)
        ones_f = sb("ones_f", [128, 128])
        ones2 = sb("ones2", [128, 128], BF16)
        epsc = sb("epsc", [128, 1])
        dma("c0", lambda q: q.dma_start(out=gv[:], in_=gpack_d[:, :]), w=["gv"])
        dma("c1", lambda q: q.dma_start(out=ident[:], in_=ident_d[:, :]), w=["ident"])
        op("pool", lambda e: e.memset(ones_bf[:], 1.0), w=["ones_bf"])
        op("pool", lambda e: e.memset(ones_f[:], 1.0), w=["ones_f"])
        op("pool", lambda e: e.memset(ones2[:], 0.0), w=["ones2"])
        op("pool", lambda e: e.memset(ones2[0:64, 0:64], 1.0), r=["ones2"], w=["ones2"])
        op("pool", lambda e: e.memset(ones2[64:128, 64:128], 1.0), r=["ones2"], w=["ones2"])
        op("pool", lambda e: e.memset(epsc[:], EPS), w=["epsc"])

        def gcol(key, c=0, rows=slice(0, 128)):
            o = goff[key] + c
            return gv[rows, o:o + 1]

        def rstd_from_ps(bank, n, dim, out_ap, key_out, rows=slice(0, 128), tmp=None, tmpkey=None):
            op("act", lambda e: e.activation(out=tmp[rows, 0:n], in_=ps[rows, bank, 0:n], func=AF.Ln, scale=1.0 / dim, bias=epsc[rows, 0:1]),
               r=[PS(bank), "epsc"], w=[tmpkey])
            op("act", lambda e: e.activation(out=out_ap, in_=tmp[rows, 0:n], func=AF.Exp, scale=-0.5), r=[tmpkey], w=[key_out])

        WSTW = 1536
        wst = [sb("wst%d" % i, [128, WSTW]) for i in range(2)]
        wcnt = [0]

        def load_w(dst, dkey, src, K, N, gain=None, scale=1.0):
            first = True
            for kc in range(K // 128):
                for n0 in range(0, N, WSTW):
                    n1 = min(N, n0 + WSTW)
                    sl = wcnt[0] % 2
                    wcnt[0] += 1
                    st = wst[sl]
                    dma("w%d" % sl, lambda q, st=st, kc=kc, n0=n0, n1=n1: q.dma_start(out=st[:, 0:n1 - n0], in_=src[kc * 128:(kc + 1) * 128, n0:n1]),
                        w=[("wst", sl)])
                    e = "dve" if (wcnt[0] % 2) else "act"
                    if gain is not None:
                        g = gcol(gain, kc)
                        if e == "dve":
                            op(e, lambda en, st=st, kc=kc, n0=n0, n1=n1, g=g: en.tensor_scalar(out=dst[:, kc, n0:n1], in0=st[:, 0:n1 - n0], scalar1=g, scalar2=None, op0=ALU.mult),
                               r=[("wst", sl), "gv"], w=[dkey])
                        else:
                            op(e, lambda en, st=st, kc=kc, n0=n0, n1=n1, g=g: en.activation(out=dst[:, kc, n0:n1], in_=st[:, 0:n1 - n0], func=AF.Copy, scale=g),
                               r=[("wst", sl), "gv"], w=[dkey])
                    else:
                        if e == "dve":
                            op(e, lambda en, st=st, kc=kc, n0=n0, n1=n1: en.tensor_copy(out=dst[:, kc, n0:n1], in_=st[:, 0:n1 - n0]),
                               r=[("wst", sl)], w=[dkey])
                        else:
                            op(e, lambda en, st=st, kc=kc, n0=n0, n1=n1: en.activation(out=dst[:, kc, n0:n1], in_=st[:, 0:n1 - n0], func=AF.Copy),
                               r=[("wst", sl)], w=[dkey])
                    first = False

        def xkeys(tag, nm, t0, t1):
            return [(tag, nm, i) for i in range(t0 // 128, (t1 - 1) // 128 + 1)]

        def norm_block(xb, xkey, hT, hkey, n, rtmp, rstd, sfx):
            op("dve", lambda e: e.tensor_tensor(out=hT[:, :, 0:n], in0=xb[:, :, 0:n], in1=xb[:, :, 0:n], op=ALU.mult), r=[xkey], w=[hkey])
            for c in range(8):
                op("pe", lambda e, c=c: e.matmul(ps[:, 7, 0:n], ones_bf[:], hT[:, c, 0:n], start=(c == 0), stop=(c == 7)), r=[hkey, "ones_bf"], w=[PS(7)])
            rstd_from_ps(7, n, 1024.0, rstd[:, 0:n], "rstd" + sfx, tmp=rtmp, tmpkey="rtmp" + sfx)
            op("dve", lambda e: e.tensor_tensor(out=hT[:, :, 0:n], in0=xb[:, :, 0:n], in1=rstd[:, 0:n].unsqueeze(1).to_broadcast([128, 8, n]), op=ALU.mult),
               r=[xkey, "rstd" + sfx, PS(7)], w=[hkey])

        with ExitStack() as st:
            tin = [sb("tin%d" % i, [128, 1024], F32, st) for i in range(2)]
            tout = [sb("tout%d" % i, [128, 8, 128], F32, st) for i in range(2)]
            msq = sb("msq", [128, 1024], F32, st)
            mtmp = [sb("mtmp%d" % i, [128, 8, 128], BF16, st) for i in range(2)]
            mss = sb("mss", [128, 1], F32, st)
            mrs = sb("mrs", [128, 1], F32, st)
            it = 0
            for nm, S in seqs:
                for tt in range(S // 128):
                    sl = it % 2
                    it += 1
                    dma("ti%d" % sl, lambda q, sl=sl, tt=tt, nm=nm: q.dma_start(out=tin[sl][:], in_=xin[nm][tt * 128:(tt + 1) * 128, :]), w=[("tin", sl)])
                    for c in range(8):
                        bk = (c // 4) + 2 * sl
                        op("pe", lambda e, c=c, bk=bk, sl=sl: e.transpose(ps[:, bk, (c % 4) * 128:(c % 4 + 1) * 128], tin[sl][:, c * 128:(c + 1) * 128], ident[:]),
                           r=[("tin", sl), "ident"], w=[PS(bk)])
                    for hh in range(2):
                        bk = hh + 2 * sl
                        eng = "dve" if hh == 0 else "act"
                        if eng == "dve":
                            op("dve", lambda e, bk=bk, hh=hh, sl=sl: e.tensor_copy(out=tout[sl][:, hh * 4:(hh + 1) * 4, :], in_=ps[:, bk, :].rearrange("p (c t) -> p c t", c=4)),
                               r=[PS(bk)], w=[("tout", sl)])
                        else:
                            op("act", lambda e, bk=bk, hh=hh, sl=sl: e.activation(out=tout[sl][:, hh * 4:(hh + 1) * 4, :], in_=ps[:, bk, :].rearrange("p (c t) -> p c t", c=4), func=AF.Copy),
                               r=[PS(bk)], w=[("tout", sl)])
                    dma("to%d" % sl, lambda q, sl=sl, tt=tt, nm=nm: q.dma_start(out=X0[nm][:, :, tt * 128:(tt + 1) * 128], in_=tout[sl][:]),
                        r=[("tout", sl)], w=[("X0", nm, tt)])
                for mt in range(2):
                    sl = it % 2
                    it += 1
                    dma("ti%d" % sl, lambda q, sl=sl, mt=mt, nm=nm: q.dma_start(out=tin[sl][:], in_=memin[nm][mt * 128:(mt + 1) * 128, :]), w=[("tin", sl)])
                    op("dve", lambda e, sl=sl: e.tensor_tensor(out=msq[:], in0=tin[sl][:], in1=tin[sl][:], op=ALU.mult), r=[("tin", sl)], w=["msq"])
                    op("dve", lambda e: e.tensor_reduce(out=mss[:], in_=msq[:], axis=mybir.AxisListType.X, op=ALU.add), r=["msq"], w=["mss"])
                    op("act", lambda e: e.activation(out=mrs[:], in_=mss[:], func=AF.Ln, scale=1.0 / 1024.0, bias=epsc[:, 0:1]), r=["mss", "epsc"], w=["mrs"])
                    op("act", lambda e: e.activation(out=mss[:], in_=mrs[:], func=AF.Exp, scale=-0.5), r=["mrs"], w=["mss"])
                    op("dve", lambda e, sl=sl: e.tensor_scalar(out=msq[:], in0=tin[sl][:], scalar1=mss[:, 0:1], scalar2=None, op0=ALU.mult), r=[("tin", sl), "mss"], w=["msq"])
                    for c in range(8):
                        bk = (c // 4) + 2 * sl
                        op("pe", lambda e, c=c, bk=bk: e.transpose(ps[:, bk, (c % 4) * 128:(c % 4 + 1) * 128], msq[:, c * 128:(c + 1) * 128], ident[:]),
                           r=["msq", "ident"], w=[PS(bk)])
                    for hh in range(2):
                        bk = hh + 2 * sl
                        op("dve", lambda e, bk=bk, hh=hh, sl=sl: e.tensor_copy(out=mtmp[sl][:, hh * 4:(hh + 1) * 4, :], in_=ps[:, bk, :].rearrange("p (c t) -> p c t", c=4)),
                           r=[PS(bk)], w=[("mtmp", sl)])
                    dma("tm%d" % sl, lambda q, sl=sl, mt=mt, nm=nm: q.dma_start(out=MEMK[nm][:, :, mt * 128:(mt + 1) * 128], in_=mtmp[sl][:]), r=[("mtmp", sl)], w=[("memn", nm)])
            mk.barrier()

        for L in range(depth):
            j = L // 2
            is_mla = (L % 2 == 0)
            with ExitStack() as st:
                xb = [sb("a_xb%d" % i, [128, 8, 512], F32, st) for i in range(2)]
                hT = sb("a_hT", [128, 8, 512], BF16, st)
                rtmp = sb("a_rtmp", [128, 512], F32, st)
                rstd = sb("a_rstd", [128, 512], F32, st)
                if is_mla:
                    wd = sb("a_wd", [128, 8, 704], BF16, st)
                    wq1 = sb("a_wq1", [128, 3, 1024], BF16, st)
                    wq2 = sb("a_wq2", [128, 3, 256], BF16, st)
                    wkn = sb("a_wkn", [128, 2, 1024], BF16, st)
                    wv = sb("a_wv", [128, 2, 512], BF16, st)
                    load_w(wd, "a_wd", W["mla_wd"][j], 1024, 704, gain=("mla_norm", j))
                    load_w(wq1, "a_wq1", W["mla_wq1"][j], 384, 1024, gain=("qlat", j))
                    load_w(wq2, "a_wq2", W["mla_wq2"][j], 384, 256, gain=("qlat", j))
                    load_w(wkn, "a_wkn", W["mla_wkn"][j], 256, 1024, gain=("kvlat", j))
                    load_w(wv, "a_wv", W["mla_wv"][j], 256, 512, gain=("kvlat", j))
                    lsq = sb("a_lsq", [128, 5, 512], BF16, st)
                    lrs = sb("a_lrs", [128, 2, 512], F32, st)
                    cqn = sb("a_cqn", [128, 3, 512], BF16, st)
                    ckvn = sb("a_ckvn", [128, 2, 512], BF16, st)
                    cosbs = [sb("a_cos%d" % i, [32, 512], F32, st) for i in range(2)]
                    sinbs = [sb("a_sin%d" % i, [32, 512], F32, st) for i in range(2)]
                    tA = sb("a_tA", [32, 512], F32, st)
                    tB = sb("a_tB", [32, 512], F32, st)
                    tK = sb("a_tK", [32, 512], F32, st)
                    sqh = [sb("a_sqh%d" % i, [128, 512], BF16, st) for i in range(2)]
                    hr = [sb("a_hr%d" % i, [128, 512], F32, st) for i in range(2)]
                    qo = [sb("a_qo%d" % i, [128, 512], BF16, st) for i in range(4)]
                    vsb = [sb("a_v%d" % i, [128, 8, 65], BF16, st) for i in range(2)]
                    for i in range(2):
                        op("pool", lambda e, i=i: e.memset(sqh[i][:], 0.0), w=[("sqh", i)])
                        op("pool", lambda e, i=i: e.memset(vsb[i][:], 1.0), w=[("vsb", i)])
                    for i in range(4):
                        op("pool", lambda e, i=i: e.memset(qo[i][:], 0.0), w=[("qo", i)])
                else:
                    wqkv = sb("a_wqkv", [128, 8, 3072], BF16, st)
                    load_w(wqkv, "a_wqkv", W["diff_wqkv"][j], 1024, 3072, gain=("diff_norm", j))
                    sqh = [sb("a_sqh%d" % i, [128, 512], BF16, st) for i in range(4)]
                    hr = [sb("a_hr%d" % i, [128, 512], F32, st) for i in range(4)]
                    qo = [sb("a_qo%d" % i, [128, 512], BF16, st) for i in range(4)]
                    vsb = [sb("a_v%d" % i, [128, 1024], BF16, st) for i in range(2)]
                qoc = 0
                vc = 0
                ablocks = [(nm, S, b) for nm, S in seqs for b in range(S // 512)]

                def a_load(bi):
                    nm, S, b = ablocks[bi]
                    sl = bi % 2
                    t0 = b * 512
                    dma("ax%d" % sl, lambda q: q.dma_start(out=xb[sl][:], in_=X0[nm][:, :, t0:t0 + 512]),
                        r=xkeys("X0", nm, t0, t0 + 512), w=[("a_xb", sl)])
                    if is_mla:
                        dma("acs%d" % sl, [lambda q: q.dma_start(out=cosbs[sl][:], in_=cos_d[:, t0:t0 + 512]),
                                           lambda q: q.dma_start(out=sinbs[sl][:], in_=ssin_d[:, t0:t0 + 512])], w=[("a_cs", sl)])

                a_load(0)
                for bi, (nm, S, b) in enumerate(ablocks):
                    if True:
                        if bi + 1 < len(ablocks):
                            a_load(bi + 1)
                        t0 = b * 512
                        n = 512
                        sl = bi % 2
                        xk = ("a_xb", sl)
                        norm_block(xb[sl], xk, hT, "a_hT", n, rtmp, rstd, "a")
                        if is_mla:
                            cosb, sinb = cosbs[sl], sinbs[sl]
                            ACS = ("a_cs", sl)
                            for m in range(5):
                                for c in range(8):
                                    op("pe", lambda e, m=m, c=c: e.matmul(ps[:, m, :], wd[:, c, m * 128:(m + 1) * 128], hT[:, c, :], start=(c == 0), stop=(c == 7)),
                                       r=["a_hT", "a_wd"], w=[PS(m)])
                            for m in range(2):
                                for c in range(8):
                                    op("pe", lambda e, m=m, c=c: e.matmul(ps[0:32, 5 + m, :], wd[:, c, 640 + 32 * m:672 + 32 * m], hT[:, c, :], start=(c == 0), stop=(c == 7)),
                                       r=["a_hT", "a_wd"], w=[PS(5 + m)])
                            for m in range(5):
                                op("act", lambda e, m=m: e.activation(out=lsq[:, m, :], in_=ps[:, m, :], func=AF.Square), r=[PS(m)], w=["a_lsq"])
                            for c in range(3):
                                op("pe", lambda e, c=c: e.matmul(ps[:, 7, :], ones_bf[:], lsq[:, c, :], start=(c == 0), stop=(c == 2)), r=["a_lsq", "ones_bf"], w=[PS(7)])
                            rstd_from_ps(7, 512, 384.0, lrs[:, 0, :], "a_lrs0", tmp=rtmp, tmpkey="rtmpa")
                            for c in range(2):
                                op("pe", lambda e, c=c: e.matmul(ps[:, 7, :], ones_bf[:], lsq[:, 3 + c, :], start=(c == 0), stop=(c == 1)), r=["a_lsq", "ones_bf"], w=[PS(7)])
                            rstd_from_ps(7, 512, 256.0, lrs[:, 1, :], "a_lrs1", tmp=rtmp, tmpkey="rtmpa")
                            for c in range(3):
                                op("dve", lambda e, c=c: e.tensor_tensor(out=cqn[:, c, :], in0=ps[:, c, :], in1=lrs[:, 0, :], op=ALU.mult), r=[PS(c), "a_lrs0"], w=["a_cqn"])
                            for c in range(2):
                                op("dve", lambda e, c=c: e.tensor_tensor(out=ckvn[:, c, :], in0=ps[:, 3 + c, :], in1=lrs[:, 1, :], op=ALU.mult), r=[PS(3 + c), "a_lrs1"], w=["a_ckvn"])
                            op("dve", lambda e: e.scalar_tensor_tensor(out=tA[:], in0=ps[0:32, 5, :], scalar=gcol(("g1k", j), 0, slice(0, 32)), in1=cosb[:], op0=ALU.mult, op1=ALU.mult),
                               r=[PS(5), "gv", ACS], w=["a_tA"])
                            op("dve", lambda e: e.scalar_tensor_tensor(out=tB[:], in0=ps[0:32, 6, :], scalar=gcol(("g2k", j), 0, slice(0, 32)), in1=sinb[:], op0=ALU.mult, op1=ALU.mult),
                               r=[PS(6), "gv", ACS], w=["a_tB"])
                            op("pool", lambda e: e.tensor_tensor(out=tK[:], in0=tA[:], in1=tB[:], op=ALU.add), r=["a_tA", "a_tB"], w=["a_tK"])
                            for side in ("k", "q"):
                                for h in range(8):
                                    pb = h % 2
                                    p1 = pb
                                    if side == "q":
                                        for c in range(3):
                                            op("pe", lambda e, c=c, h=h, p1=p1: e.matmul(ps[:, p1, :], wq1[:, c, h * 128:(h + 1) * 128], cqn[:, c, :], start=(c == 0), stop=(c == 2)),
                                               r=["a_cqn", "a_wq1"], w=[PS(p1)])
                                        p2 = 2 + pb
                                        for c in range(3):
                                            op("pe", lambda e, c=c, h=h, p2=p2: e.matmul(ps[0:32, p2, :], wq2[:, c, h * 32:(h + 1) * 32], cqn[:, c, :], start=(c == 0), stop=(c == 2)),
                                               r=["a_cqn", "a_wq2"], w=[PS(p2)])
                                        op("act", lambda e, p1=p1, pb=pb: e.activation(out=sqh[pb][:], in_=ps[:, p1, :], func=AF.Square), r=[PS(p1)], w=[("sqh", pb)])
                                    else:
                                        for c in range(2):
                                            op("pe", lambda e, c=c, h=h, p1=p1: e.matmul(ps[:, p1, :], wkn[:, c, h * 128:(h + 1) * 128], ckvn[:, c, :], start=(c == 0), stop=(c == 1)),
                                               r=["a_ckvn", "a_wkn"], w=[PS(p1)])
                                        op("act", lambda e, p1=p1, pb=pb: e.activation(out=sqh[pb][64:128, :], in_=ps[64:128, p1, :], func=AF.Square), r=[PS(p1)], w=[("sqh", pb)])
                                        op("act", lambda e, pb=pb: e.activation(out=sqh[pb][0:32, :], in_=ps[0:32, 5, :], func=AF.Square), r=[PS(5), ("sqh", pb)], w=[("sqh", pb)])
                                    sbk = 4 if pb == 0 else 7
                                    op("pe", lambda e, pb=pb, sbk=sbk: e.matmul(ps[:, sbk, :], ones_bf[:], sqh[pb][:], start=True, stop=True), r=[("sqh", pb), "ones_bf"], w=[PS(sbk)])
                                    rstd_from_ps(sbk, 512, 96.0, hr[pb][:], ("hr", pb), tmp=rtmp, tmpkey="rtmpa")
                                    qs = qoc % 4
                                    qoc += 1
                                    g1 = ("g1" + side, j)
                                    if side == "q":
                                        op("dve", lambda e, p1=p1, g1=g1: e.scalar_tensor_tensor(out=tA[:], in0=ps[0:32, p1, :], scalar=gcol(g1, 0, slice(0, 32)), in1=cosb[:], op0=ALU.mult, op1=ALU.mult),
                                           r=[PS(p1), "gv", ACS], w=["a_tA"])
                                        op("dve", lambda e, p2=p2: e.scalar_tensor_tensor(out=tB[:], in0=ps[0:32, p2, :], scalar=gcol(("g2q", j), 0, slice(0, 32)), in1=sinb[:], op0=ALU.mult, op1=ALU.mult),
                                           r=[PS(p2), "gv", ACS], w=["a_tB"])
                                        op("pool", lambda e: e.tensor_tensor(out=tA[:], in0=tA[:], in1=tB[:], op=ALU.add), r=["a_tA", "a_tB"], w=["a_tA"])
                                        op("pool", lambda e, qs=qs, pb=pb: e.tensor_tensor(out=qo[qs][0:32, :], in0=tA[:], in1=hr[pb][0:32, :], op=ALU.mult), r=["a_tA", ("hr", pb)], w=[("qo", qs)])
                                    else:
                                        op("pool", lambda e, qs=qs, pb=pb: e.tensor_tensor(out=qo[qs][0:32, :], in0=tK[:], in1=hr[pb][0:32, :], op=ALU.mult), r=["a_tK", ("hr", pb)], w=[("qo", qs)])
                                    op("dve", lambda e, p1=p1, qs=qs, pb=pb, g1=g1: e.scalar_tensor_tensor(out=qo[qs][64:128, :], in0=ps[64:128, p1, :], scalar=gcol(g1, 0, slice(64, 128)), in1=hr[pb][64:128, :], op0=ALU.mult, op1=ALU.mult),
                                       r=[PS(p1), "gv", ("hr", pb), ("qo", qs)], w=[("qo", qs)])
                                    dst = (QT if side == "q" else KT)[nm]
                                    dma("aq%d" % qs, lambda q, qs=qs, dst=dst, h=h, t0=t0: q.dma_start(out=dst[h, :, t0:t0 + 512], in_=qo[qs][:]),
                                        r=[("qo", qs)], w=[("QK", side, nm, h, b)])
                            for tt in range(4):
                                vs = vc % 2
                                vc += 1
                                for c in range(2):
                                    op("pe", lambda e, c=c, tt=tt: e.matmul(ps[:, 4, :], ckvn[:, c, tt * 128:(tt + 1) * 128], wv[:, c, :], start=(c == 0), stop=(c == 1)),
                                       r=["a_ckvn", "a_wv"], w=[PS(4)])
                                op("act", lambda e, vs=vs: e.activation(out=vsb[vs][:, :, 0:64], in_=ps[:, 4, :].rearrange("p (h d) -> p h d", h=8), func=AF.Copy), r=[PS(4)], w=[("vsb", vs)])
                                dma("av%d" % vs, lambda q, vs=vs, nm=nm, T=b * 4 + tt: q.dma_start(out=VD[nm][:, :, T, 0:65].rearrange("h p d -> p h d"), in_=vsb[vs][:]),
                                    r=[("vsb", vs)], w=[("VD", nm, b)])
                        else:
                            for side, cb in (("q", 0), ("k", 1024)):
                                for h in range(8):
                                    pb = h % 4
                                    p1 = pb
                                    for c in range(8):
                                        op("pe", lambda e, c=c, h=h, p1=p1, cb=cb: e.matmul(ps[:, p1, :], wqkv[:, c, cb + h * 128:cb + (h + 1) * 128], hT[:, c, :], start=(c == 0), stop=(c == 7)),
                                           r=["a_hT", "a_wqkv"], w=[PS(p1)])
                                    op("act", lambda e, p1=p1, pb=pb: e.activation(out=sqh[pb][:], in_=ps[:, p1, :], func=AF.Square), r=[PS(p1)], w=[("sqh", pb)])
                                    sbk = 4 + pb
                                    op("pe", lambda e, pb=pb, sbk=sbk: e.matmul(ps[:, sbk, :], ones2[:], sqh[pb][:], start=True, stop=True), r=[("sqh", pb), "ones2"], w=[PS(sbk)])
                                    rstd_from_ps(sbk, 512, 64.0, hr[pb][:], ("hr", pb), tmp=rtmp, tmpkey="rtmpa")
                                    qs = qoc % 4
                                    qoc += 1
                                    gk = ("d" + side, j)
                                    op("dve", lambda e, p1=p1, qs=qs, pb=pb, gk=gk: e.scalar_tensor_tensor(out=qo[qs][:], in0=ps[:, p1, :], scalar=gcol(gk), in1=hr[pb][:], op0=ALU.mult, op1=ALU.mult),
                                       r=[PS(p1), "gv", ("hr", pb)], w=[("qo", qs)])
                                    dst = (QT if side == "q" else KT)[nm]
                                    dma("aq%d" % qs, lambda q, qs=qs, dst=dst, h=h, t0=t0: q.dma_start(out=dst[h, :, t0:t0 + 512], in_=qo[qs][:]),
                                        r=[("qo", qs)], w=[("QK", side, nm, h, b)])
                            for tt in range(4):
                                vs = vc % 2
                                vc += 1
                                for half in range(2):
                                    bk = 2 + half
                                    for c in range(8):
                                        op("pe", lambda e, c=c, tt=tt, half=half, bk=bk: e.matmul(ps[:, bk, :], hT[:, c, tt * 128:(tt + 1) * 128], wqkv[:, c, 2048 + half * 512:2048 + (half + 1) * 512], start=(c == 0), stop=(c == 7)),
                                           r=["a_hT", "a_wqkv"], w=[PS(bk)])
                                op("act", lambda e, vs=vs: e.activation(out=vsb[vs][:], in_=ps[:, 2:4, :].rearrange("p a b -> p (a b)"), func=AF.Copy), r=[PS(2), PS(3)], w=[("vsb", vs)])
                                dma("av%d" % vs, lambda q, vs=vs, nm=nm, T=b * 4 + tt: q.dma_start(out=VD[nm][:, :, T, 0:128].rearrange("h p d -> p h d"), in_=vsb[vs][:].rearrange("p (h d) -> p h d", h=8)),
                                    r=[("vsb", vs)], w=[("VD", nm, b)])
                mk.barrier()

            with ExitStack() as st:
                heads = [(nm, S, h) for nm, S in seqs for h in range(8)]
                steps = []

                def run_pipeline(LA):
                    nst = len(steps)
                    for i in range(nst + LA):
                        if i < nst:
                            for f in steps[i]["pre"]:
                                f()
                            steps[i]["qk"]()
                            steps[i]["ex"]()
                        if i >= LA:
                            steps[i - LA]["pv"]()
                            for f in steps[i - LA]["post"]:
                                f()

                if is_mla:
                    kt = [sb("b_kt%d" % i, [128, smax], BF16, st) for i in range(2)]
                    vt = [sb("b_vt%d" % i, [128, smax // 128, 65], BF16, st) for i in range(2)]
                    qb = [sb("b_q%d" % i, [128, 512], BF16, st) for i in range(2)]
                    pt = [sb("b_p%d" % i, [128, 1024], BF16, st) for i in range(3)]
                    osb = sb("b_osb", [128, 512], F32, st)
                    rr = sb("b_rr", [128, 512], F32, st)
                    oo = [sb("b_oo%d" % i, [64, 512], BF16, st) for i in range(2)]

                    def load_head(hi):
                        nm, S, h = heads[hi]
                        hs = hi % 2
                        nkt = S // 128
                        dma("bk%d" % hs, lambda q: q.dma_start(out=kt[hs][:, 0:S], in_=KT[nm][h, :, :]),
                            r=[("QK", "k", nm, h, bb) for bb in range(S // 512)], w=[("b_kt", hs)])
                        dma("bv%d" % hs, lambda q: q.dma_start(out=vt[hs][:, 0:nkt, :], in_=VD[nm][h, :, :, 0:65]),
                            r=[("VD", nm, bb) for bb in range(S // 512)], w=[("b_vt", hs)])

                    qblocks = [(hi, b) for hi, (nm, S, h) in enumerate(heads) for b in range(S // 512)]

                    def load_q(qi):
                        hi, b = qblocks[qi]
                        nm, S, h = heads[hi]
                        qs = qi % 2
                        dma("bq%d" % qs, lambda q: q.dma_start(out=qb[qs][:], in_=QT[nm][h, :, b * 512:(b + 1) * 512]),
                            r=[("QK", "q", nm, h, b)], w=[("b_q", qs)])

                    gstep = 0
                    for qi, (hi, b) in enumerate(qblocks):
                        nm, S, h = heads[hi]
                        hs = hi % 2
                        qs = qi % 2
                        nkt = S // 128
                        npair = nkt // 2
                        ob = 6 + (qi % 2)
                        os_ = qi % 2
                        t0 = b * 512
                        for kp in range(npair):
                            pre = []
                            if kp == 0:
                                if qi == 0:
                                    pre.append(lambda: load_head(0))
                                    if len(heads) > 1:
                                        pre.append(lambda: load_head(1))
                                    pre.append(lambda: load_q(0))
                                if qi + 1 < len(qblocks):
                                    pre.append(lambda qi=qi: load_q(qi + 1))
                            sl3 = gstep % 3
                            gstep += 1
                            sbk = 2 * sl3

                            def qk(hs=hs, qs=qs, kp=kp, sbk=sbk):
                                for u in range(2):
                                    k0 = (2 * kp + u) * 128
                                    op("pe", lambda e, k0=k0, bk=sbk + u: e.matmul(ps[:, bk, :], kt[hs][:, k0:k0 + 128], qb[qs][:], start=True, stop=True),
                                       r=[("b_kt", hs), ("b_q", qs)], w=[PS(sbk + u)])

                            def ex(sbk=sbk, sl3=sl3):
                                op("act", lambda e: e.activation(out=pt[sl3][:], in_=ps[:, sbk:sbk + 2, :].rearrange("p a b -> p (a b)"), func=AF.Exp, scale=96.0 ** -0.5),
                                   r=[PS(sbk), PS(sbk + 1)], w=[("b_p", sl3)])

                            def pv(hs=hs, kp=kp, sl3=sl3, ob=ob, nkt=nkt):
                                for u in range(2):
                                    kk = 2 * kp + u
                                    op("pe", lambda e, kk=kk, u=u: e.matmul(ps[0:65, ob, :], vt[hs][:, kk, :], pt[sl3][:, u * 512:(u + 1) * 512], start=(kk == 0), stop=(kk == nkt - 1)),
                                       r=[("b_vt", hs), ("b_p", sl3)], w=[PS(ob)])

                            post = []
                            if kp == npair - 1:
                                def fin(ob=ob, os_=os_, nm=nm, h=h, t0=t0, b=b):
                                    op("dve", lambda e: e.tensor_copy(out=osb[0:65, :], in_=ps[0:65, ob, :]), r=[PS(ob)], w=["b_osb"])
                                    op("dve", lambda e: e.reciprocal(out=rr[64:65, :], in_=osb[64:65, :]), r=["b_osb"], w=["b_rr"])
                                    op("pe", lambda e: e.matmul(ps[0:64, ob, :], ones_f[64:65, 0:64], rr[64:65, :], start=True, stop=True), r=["b_rr", "ones_f", "b_osb"], w=[PS(ob)])
                                    op("dve", lambda e: e.tensor_tensor(out=oo[os_][:], in0=osb[0:64, :], in1=ps[0:64, ob, :], op=ALU.mult), r=["b_osb", PS(ob)], w=[("b_oo", os_)])
                                    dma("bo%d" % os_, lambda q: q.dma_start(out=OT[nm][h // 2, (h % 2) * 64:(h % 2) * 64 + 64, t0:t0 + 512], in_=oo[os_][:]),
                                        r=[("b_oo", os_)], w=[("OT", nm, b, h)])
                                post.append(fin)
                                if b == S // 512 - 1 and hi + 2 < len(heads):
                                    post.append(lambda hi=hi: load_head(hi + 2))
                            steps.append(dict(pre=pre, qk=qk, ex=ex, pv=pv, post=post))
                    run_pipeline(2)
                else:
                    kt = [sb("b_kt%d" % i, [68, 2, smax], BF16, st) for i in range(2)]
                    vt = [sb("b_vt%d" % i, [128, smax // 128, 128], BF16, st) for i in range(2)]
                    qL = [sb("b_qL%d" % i, [68, 2, 512], BF16, st) for i in range(2)]
                    qR = [sb("b_qR%d" % i, [68, 2, 512], BF16, st) for i in range(2)]
                    pt = [sb("b_p%d" % i, [128, 1024], BF16, st) for i in range(3)]
                    adg = sb("b_adg", [128, 4, 512], F32, st)
                    r0 = sb("b_r0", [128, 2, 512], F32, st)
                    sacc = [sb("b_sacc%d" % i, [128, 2, 512], F32, st) for i in range(2)]
                    o0 = sb("b_o0", [128, 2, 512], F32, st)
                    osq = sb("b_osq", [128, 512], BF16, st)
                    ors = sb("b_ors", [128, 512], F32, st)
                    ortmp = sb("b_ortmp", [128, 512], F32, st)
                    oo = [sb("b_oo%d" % i, [128, 512], BF16, st) for i in range(2)]
                    lamt = sb("b_lamt", [1, 256], F32, st)
                    lamp = sb("b_lamp", [1, 128], F32, st)
                    lams = sb("b_lams", [1, 4], F32, st)
                    nlam = sb("b_nlam", [128, 1], F32, st)
                    gsub = sb("b_gsub", [128, 1], F32, st)
                    lambda_init = 0.8 - 0.6 * math.exp(-0.3 * L)
                    dma("c0", lambda q: q.dma_start(out=adg[:], in_=ald_d[:, :, :]), w=["b_adg"])
                    for i in range(2):
                        dma("c1", [lambda q, i=i, mm=mm: q.dma_start(out=qL[i][64:68, mm, :], in_=alq_d[0:4, :]) for mm in range(2)] +
                                  [lambda q, i=i, mm=mm: q.dma_start(out=qR[i][64:68, mm, :], in_=alq_d[4:8, :]) for mm in range(2)], w=[("b_qa", i)])
                    dma("c0", lambda q: q.dma_start(out=lamt[:], in_=W["diff_lam"][j]), w=["b_lamt"])
                    op("dve", lambda e: e.tensor_tensor(out=lamp[:].rearrange("p (a d) -> p a d", a=2), in0=lamt[:].rearrange("p (a t d) -> p a t d", a=2, t=2)[:, :, 0, :],
                                                        in1=lamt[:].rearrange("p (a t d) -> p a t d", a=2, t=2)[:, :, 1, :], op=ALU.mult), r=["b_lamt"], w=["b_lamp"])
                    op("dve", lambda e: e.tensor_reduce(out=lams[:, 0:2], in_=lamp[:].rearrange("p (a d) -> p a d", a=2), axis=mybir.AxisListType.X, op=ALU.add), r=["b_lamp"], w=["b_lams"])
                    op("act", lambda e: e.activation(out=lams[:, 2:4], in_=lams[:, 0:2], func=AF.Exp), r=["b_lams"], w=["b_lams2"])
                    op("dve", lambda e: e.scalar_tensor_tensor(out=lams[:, 0:1], in0=lams[:, 3:4], scalar=-lambda_init, in1=lams[:, 2:3], op0=ALU.add, op1=ALU.subtract), r=["b_lams2", "b_lams"], w=["b_lams"])
                    op("pe", lambda e: e.matmul(ps[:, 0, 0:1], ones_f[0:1, :], lams[0:1, 0:1], start=True, stop=True), r=["b_lams", "ones_f"], w=[PS(0)])
                    op("dve", lambda e: e.tensor_copy(out=nlam[:], in_=ps[:, 0, 0:1]), r=[PS(0)], w=["b_nlam"])
                    op("dve", lambda e: e.tensor_scalar(out=gsub[:], in0=gcol(("dsub", j)), scalar1=float(1.0 - lambda_init), scalar2=None, op0=ALU.mult), r=["gv"], w=["b_gsub"])

                    def load_head(hi):
                        nm, S, h = heads[hi]
                        hs = hi % 2
                        nkt = S // 128
                        dma("bk%d" % hs, [lambda q, mm=mm: q.dma_start(out=kt[hs][0:64, mm, 0:S], in_=KT[nm][h, mm * 64:(mm + 1) * 64, :]) for mm in range(2)] +
                                         [lambda q, mm=mm: q.dma_start(out=kt[hs][64:68, mm, 0:S], in_=alk_d[h, :, 0:S]) for mm in range(2)],
                            r=[("QK", "k", nm, h, bb) for bb in range(S // 512)], w=[("b_kt", hs)])
                        dma("bv%d" % hs, lambda q: q.dma_start(out=vt[hs][:, 0:nkt, :], in_=VD[nm][h, :, :, 0:128]),
                            r=[("VD", nm, bb) for bb in range(S // 512)], w=[("b_vt", hs)])

                    qblocks = [(hi, b) for hi, (nm, S, h) in enumerate(heads) for b in range(S // 512)]

                    def load_q(qi):
                        hi, b = qblocks[qi]
                        nm, S, h = heads[hi]
                        qs = qi % 2
                        dma("bq%d" % qs, [lambda q, mm=mm, dst=dst: q.dma_start(out=dst[qs][0:64, mm, :], in_=QT[nm][h, mm * 64:(mm + 1) * 64, b * 512:(b + 1) * 512])
                                          for mm in range(2) for dst in (qL, qR)],
                            r=[("QK", "q", nm, h, b)], w=[("b_q", qs)])

                    BAND_T = 50.0
                    gstep = 0
                    for qi, (hi, b) in enumerate(qblocks):
                        nm, S, h = heads[hi]
                        hs = hi % 2
                        qs = qi % 2
                        nkt = S // 128
                        os_ = qi % 2
                        t0 = b * 512
                        slope = 2.0 ** (-(h + 1))
                        incl = []
                        for kk in range(nkt):
                            kb = kk // 4
                            if kb < b:
                                dmin = t0 - (kk * 128 + 127)
                            elif kb > b:
                                dmin = kk * 128 - (t0 + 511)
                            else:
                                dmin = 0
                            if slope * dmin <= BAND_T:
                                incl.append(kk)
                        for ii, kk in enumerate(incl):
                            first = (ii == 0)
                            last = (ii == len(incl) - 1)
                            pre = []
                            if first:
                                if qi == 0:
                                    pre.append(lambda: load_head(0))
                                    if len(heads) > 1:
                                        pre.append(lambda: load_head(1))
                                    pre.append(lambda: load_q(0))
                                if qi + 1 < len(qblocks):
                                    pre.append(lambda qi=qi: load_q(qi + 1))
                            sl2 = gstep % 2
                            sl3 = gstep % 3
                            gstep += 1
                            ssl = 2 * sl2
                            kb = kk // 4
                            k0 = kk * 128

                            def qk(hs=hs, qs=qs, kk=kk, kb=kb, k0=k0, ssl=ssl, b=b, slope=slope):
                                for mm in range(2):
                                    bk = ssl + mm
                                    if kb == b:
                                        op("pe", lambda e, bk=bk, mm=mm: e.matmul(ps[:, bk, :], kt[hs][0:64, mm, k0:k0 + 128], qL[qs][0:64, mm, :], start=True, stop=True),
                                           r=[("b_kt", hs), ("b_q", qs)], w=[PS(bk)])
                                        op("dve", lambda e, bk=bk, jj=kk % 4: e.scalar_tensor_tensor(out=ps[:, bk, :], in0=adg[:, jj, :], scalar=float(-slope * 8.0), in1=ps[:, bk, :], op0=ALU.mult, op1=ALU.add),
                                           r=["b_adg", PS(bk)], w=[PS(bk)])
                                    else:
                                        qsrc = qL if kb < b else qR
                                        op("pe", lambda e, bk=bk, mm=mm, qsrc=qsrc: e.matmul(ps[:, bk, :], kt[hs][:, mm, k0:k0 + 128], qsrc[qs][:, mm, :], start=True, stop=True),
                                           r=[("b_kt", hs), ("b_q", qs), ("b_qa", qs)], w=[PS(bk)])

                            def ex(ssl=ssl, sl3=sl3, kb=kb, b=b, slope=slope):
                                cb = float(-slope * 512.0 * abs(b - kb))
                                for mm in range(2):
                                    op("act", lambda e, mm=mm: e.activation(out=pt[sl3][:, mm * 512:(mm + 1) * 512], in_=ps[:, ssl + mm, :], func=AF.Exp, scale=0.125, bias=cb),
                                       r=[PS(ssl + mm)], w=[("b_p", sl3, mm)])

                            def pv(hs=hs, kk=kk, sl3=sl3, first=first, last=last, asl=qi % 2):
                                for mm in range(2):
                                    op("pe", lambda e, mm=mm: e.matmul(ps[:, 4 + mm, :], vt[hs][:, kk, :], pt[sl3][:, mm * 512:(mm + 1) * 512], start=first, stop=last),
                                       r=[("b_vt", hs), ("b_p", sl3, mm)], w=[PS(4 + mm)])
                                    if mm == 0:
                                        op("pe", lambda e, mm=mm: e.matmul(ps[:, 6, :], ones_bf[:], pt[sl3][:, 0:512], start=first, stop=last),
                                           r=["ones_bf", ("b_p", sl3, mm)], w=[PS(6)])
                                    elif first:
                                        op("dve", lambda e, mm=mm: e.tensor_copy(out=sacc[asl][:, mm, :], in_=pt[sl3][:, mm * 512:(mm + 1) * 512]), r=[("b_p", sl3, mm)], w=[("b_sacc", asl, mm)])
                                    else:
                                        op("dve", lambda e, mm=mm: e.tensor_tensor(out=sacc[asl][:, mm, :], in0=sacc[asl][:, mm, :], in1=pt[sl3][:, mm * 512:(mm + 1) * 512], op=ALU.add),
                                           r=[("b_p", sl3, mm), ("b_sacc", asl, mm)], w=[("b_sacc", asl, mm)])

                            post = []
                            if last:
                                def fin(os_=os_, nm=nm, h=h, t0=t0, b=b, asl=qi % 2):
                                    op("dve", lambda e: e.tensor_copy(out=o0[:].rearrange("p a b -> p (a b)"), in_=ps[:, 4:6, :].rearrange("p a b -> p (a b)")), r=[PS(4), PS(5)], w=["b_o0"])
                                    op("pe", lambda e: e.matmul(ps[:, 7, :], ones_f[:], sacc[asl][:, 1, :], start=True, stop=True), r=["ones_f", ("b_sacc", asl, 1)], w=[PS(7)])
                                    op("dve", lambda e: e.reciprocal(out=r0[:].rearrange("p a b -> p (a b)"), in_=ps[:, 6:8, :].rearrange("p a b -> p (a b)")), r=[PS(6), PS(7)], w=["b_r0"])
                                    op("dve", lambda e: e.tensor_tensor(out=o0[:].rearrange("p a b -> p (a b)"), in0=o0[:].rearrange("p a b -> p (a b)"), in1=r0[:].rearrange("p a b -> p (a b)"), op=ALU.mult),
                                       r=["b_o0", "b_r0"], w=["b_o0"])
                                    op("dve", lambda e: e.scalar_tensor_tensor(out=o0[:, 0, :], in0=o0[:, 1, :], scalar=nlam[:, 0:1], in1=o0[:, 0, :], op0=ALU.mult, op1=ALU.add), r=["b_o0", "b_nlam"], w=["b_o0"])
                                    op("pool", lambda e: e.tensor_tensor(out=osq[:], in0=o0[:, 0, :], in1=o0[:, 0, :], op=ALU.mult), r=["b_o0"], w=["b_osq"])
                                    op("pe", lambda e: e.matmul(ps[:, 7, :], ones_bf[:], osq[:], start=True, stop=True), r=["b_osq", "ones_bf", "b_r0"], w=[PS(7)])
                                    rstd_from_ps(7, 512, 128.0, ors[:], "b_ors", tmp=ortmp, tmpkey="b_ortmp")
                                    op("dve", lambda e: e.scalar_tensor_tensor(out=oo[os_][:], in0=o0[:, 0, :], scalar=gsub[:, 0:1], in1=ors[:], op0=ALU.mult, op1=ALU.mult),
                                       r=["b_o0", "b_gsub", "b_ors", PS(7)], w=[("b_oo", os_)])
                                    dma("bo%d" % os_, lambda q: q.dma_start(out=OT[nm][h, :, t0:t0 + 512], in_=oo[os_][:]),
                                        r=[("b_oo", os_)], w=[("OT", nm, b, h)])
                                post.append(fin)
                                if b == S // 512 - 1 and hi + 2 < len(heads):
                                    post.append(lambda hi=hi: load_head(hi + 2))
                            steps.append(dict(pre=pre, qk=qk, ex=ex, pv=pv, post=post))
                    run_pipeline(1)
                mk.barrier()

            with ExitStack() as st:
                nko = 4 if is_mla else 8
                wo = sb("c_wo", [128, nko, 1024], BF16, st)
                load_w(wo, "c_wo", (W["mla_wo"] if is_mla else W["diff_wo"])[j], nko * 128, 1024)
                wxq = sb("c_wxq", [128, 8, 1024], BF16, st)
                wxo = sb("c_wxo", [128, 8, 1024], BF16, st)
                load_w(wxq, "c_wxq", W["xa_wq"][L], 1024, 1024, gain=("xa_norm", L))
                load_w(wxo, "c_wxo", W["xa_wo"][L], 1024, 1024)
                mkT = {nm: sb("c_mk_" + nm, [128, 8, NMEM], BF16, st) for nm, _ in seqs}
                mv = {nm: sb("c_mv_" + nm, [128, 2, 1024], BF16, st) for nm, _ in seqs}
                hr = sb("c_hr", [128, 512], F32, st)
                rtmp = sb("c_rtmp", [128, 512], F32, st)
                with ExitStack() as st2:
                    wkv = sb("c_wkv", [128, 8, 2048], BF16, st2)
                    load_w(wkv, "c_wkv", W["xa_wkv"][L], 1024, 2048, gain=("xa_mem", L))
                    ksq = sb("c_ksq", [128, 2, NMEM], BF16, st2)
                    memn = {nm: sb("c_memn_" + nm, [128, 8, NMEM], BF16, st2) for nm, _ in seqs}
                    for mi, (nm, _) in enumerate(seqs):
                        dma("cm%d" % (mi % 2), lambda q, nm=nm: q.dma_start(out=memn[nm][:], in_=MEMK[nm][:, :, :]), r=[("memn", nm)], w=[("c_memn", nm)])
                    for nm, _ in seqs:
                        for hd in range(4):
                            for dc in range(2):
                                m = hd * 2 + dc
                                for c in range(8):
                                    op("pe", lambda e, c=c, m=m, dc=dc, nm=nm: e.matmul(ps[:, dc, 0:NMEM], wkv[:, c, m * 128:(m + 1) * 128], memn[nm][:, c, :], start=(c == 0), stop=(c == 7)),
                                       r=["c_wkv", ("c_memn", nm)], w=[PS(dc)])
                                op("act", lambda e, dc=dc: e.activation(out=ksq[:, dc, :], in_=ps[:, dc, 0:NMEM], func=AF.Square), r=[PS(dc)], w=["c_ksq"])
                            for dc in range(2):
                                op("pe", lambda e, dc=dc: e.matmul(ps[:, 2, 0:NMEM], ones_bf[:], ksq[:, dc, :], start=(dc == 0), stop=(dc == 1)), r=["c_ksq", "ones_bf"], w=[PS(2)])
                            rstd_from_ps(2, NMEM, 256.0, hr[:, 0:NMEM], "c_hr", tmp=rtmp, tmpkey="rtmpc")
                            for dc in range(2):
                                op("dve", lambda e, dc=dc, hd=hd, nm=nm: e.scalar_tensor_tensor(out=mkT[nm][:, hd * 2 + dc, :], in0=ps[:, dc, 0:NMEM], scalar=gcol(("xk", L), dc), in1=hr[:, 0:NMEM], op0=ALU.mult, op1=ALU.mult),
                                   r=[PS(dc), "gv", "c_hr"], w=[("c_mk", nm)])
                        for ktile in range(2):
                            for half in range(2):
                                for c in range(8):
                                    op("pe", lambda e, c=c, ktile=ktile, half=half, nm=nm: e.matmul(ps[:, 4 + half, :], memn[nm][:, c, ktile * 128:(ktile + 1) * 128], wkv[:, c, 1024 + half * 512:1024 + (half + 1) * 512], start=(c == 0), stop=(c == 7)),
                                       r=["c_wkv", ("c_memn", nm)], w=[PS(4 + half)])
                            op("act", lambda e, ktile=ktile, nm=nm: e.activation(out=mv[nm][:, ktile, :], in_=ps[:, 4:6, :].rearrange("p a b -> p (a b)"), func=AF.Copy), r=[PS(4), PS(5)], w=[("c_mv", nm)])
                    mk.barrier()
                xb = [sb("c_xb%d" % i, [128, 8, 512], F32, st) for i in range(2)]
                ob = [sb("c_ob%d" % i, [128, nko, 512], BF16, st) for i in range(2)]
                hT = sb("c_hT", [128, 8, 512], BF16, st)
                rstd = sb("c_rstd", [128, 512], F32, st)
                qsq = sb("c_qsq", [128, 2, 512], BF16, st)
                qn2 = [sb("c_qn%d" % i, [128, 2, 512], BF16, st) for i in range(2)]
                pt = [sb("c_p%d" % i, [128, 2, 512], BF16, st) for i in range(2)]
                rs = sb("c_rs", [128, 512], F32, st)
                on = sb("c_on", [128, 8, 512], BF16, st)
                cblocks = [(nm, S, b) for nm, S in seqs for b in range(S // 512)]

                def c_load(bi):
                    nm, S, b = cblocks[bi]
                    sl = bi % 2
                    t0 = b * 512
                    dma("cx%d" % sl, lambda q: q.dma_start(out=xb[sl][:], in_=X0[nm][:, :, t0:t0 + 512]), r=xkeys("X0", nm, t0, t0 + 512), w=[("c_xb", sl)])
                    dma("co%d" % sl, lambda q: q.dma_start(out=ob[sl][:], in_=OT[nm][0:nko, :, t0:t0 + 512].rearrange("c p t -> p c t")),
                        r=[("OT", nm, b, h) for h in range(8)], w=[("c_ob", sl)])

                c_load(0)
                for bi, (nm, S, b) in enumerate(cblocks):
                    if True:
                        if bi + 1 < len(cblocks):
                            c_load(bi + 1)
                        t0 = b * 512
                        sl = bi % 2
                        xk = ("c_xb", sl)
                        for m in range(8):
                            bk = m % 2
                            for c in range(nko):
                                op("pe", lambda e, c=c, m=m, bk=bk, sl=sl: e.matmul(ps[:, bk, :], wo[:, c, m * 128:(m + 1) * 128], ob[sl][:, c, :], start=(c == 0), stop=(c == nko - 1)),
                                   r=["c_wo", ("c_ob", sl)], w=[PS(bk)])
                            op("dve", lambda e, m=m, bk=bk, sl=sl: e.tensor_tensor(out=xb[sl][:, m, :], in0=xb[sl][:, m, :], in1=ps[:, bk, :], op=ALU.add), r=[xk, PS(bk)], w=[xk])
                        norm_block(xb[sl], xk, hT, "c_hT", 512, rtmp, rstd, "c")
                        def xa_s1(hd):
                            qb_ = hd % 2
                            for dc in range(2):
                                m = hd * 2 + dc
                                for c in range(8):
                                    op("pe", lambda e, c=c, m=m, dc=dc: e.matmul(ps[:, dc, :], wxq[:, c, m * 128:(m + 1) * 128], hT[:, c, :], start=(c == 0), stop=(c == 7)),
                                       r=["c_wxq", "c_hT"], w=[PS(dc)])
                            op("act", lambda e: e.activation(out=qsq[:].rearrange("p a b -> p (a b)"), in_=ps[:, 0:2, :].rearrange("p a b -> p (a b)"), func=AF.Square), r=[PS(0), PS(1)], w=["c_qsq"])
                            for dc in range(2):
                                op("pe", lambda e, dc=dc: e.matmul(ps[:, 2, :], ones_bf[:], qsq[:, dc, :], start=(dc == 0), stop=(dc == 1)), r=["c_qsq", "ones_bf"], w=[PS(2)])
                            rstd_from_ps(2, 512, 256.0, hr[:], "c_hr", tmp=rtmp, tmpkey="rtmpc")
                            for dc in range(2):
                                op("dve", lambda e, dc=dc: e.scalar_tensor_tensor(out=qn2[qb_][:, dc, :], in0=ps[:, dc, :], scalar=gcol(("xq", L), dc), in1=hr[:], op0=ALU.mult, op1=ALU.mult),
                                   r=[PS(dc), "gv", "c_hr"], w=[("c_qn", qb_)])

                        def xa_s2(hd, nm=nm):
                            qb_ = hd % 2
                            psl = hd % 2
                            for ktile in range(2):
                                for dc in range(2):
                                    op("pe", lambda e, ktile=ktile, dc=dc: e.matmul(ps[:, 3 + ktile, :], mkT[nm][:, hd * 2 + dc, ktile * 128:(ktile + 1) * 128], qn2[qb_][:, dc, :], start=(dc == 0), stop=(dc == 1)),
                                       r=[("c_mk", nm), ("c_qn", qb_)], w=[PS(3 + ktile)])
                            op("act", lambda e: e.activation(out=pt[psl][:].rearrange("p a b -> p (a b)"), in_=ps[:, 3:5, :].rearrange("p a b -> p (a b)"), func=AF.Exp, scale=1.0 / 16.0),
                               r=[PS(3), PS(4)], w=[("c_p", psl)])
                            for ktile in range(2):
                                op("pe", lambda e, ktile=ktile: e.matmul(ps[:, 5, :], ones_bf[:], pt[psl][:, ktile, :], start=(ktile == 0), stop=(ktile == 1)), r=[("c_p", psl), "ones_bf"], w=[PS(5)])
                            for dc in range(2):
                                for ktile in range(2):
                                    op("pe", lambda e, ktile=ktile, dc=dc: e.matmul(ps[:, 6 + dc, :], mv[nm][:, ktile, hd * 256 + dc * 128:hd * 256 + (dc + 1) * 128], pt[psl][:, ktile, :], start=(ktile == 0), stop=(ktile == 1)),
                                       r=[("c_mv", nm), ("c_p", psl)], w=[PS(6 + dc)])
                            op("dve", lambda e: e.reciprocal(out=rs[:], in_=ps[:, 5, :]), r=[PS(5)], w=["c_rs"])
                            for dc in range(2):
                                op("dve", lambda e, dc=dc: e.tensor_tensor(out=on[:, hd * 2 + dc, :], in0=ps[:, 6 + dc, :], in1=rs[:], op=ALU.mult), r=[PS(6 + dc), "c_rs"], w=["c_on"])

                        xa_s1(0)
                        for hd in range(4):
                            if hd + 1 < 4:
                                xa_s1(hd + 1)
                            xa_s2(hd)
                        for m in range(8):
                            bk = m % 2
                            for c in range(8):
                                op("pe", lambda e, c=c, m=m, bk=bk: e.matmul(ps[:, bk, :], wxo[:, c, m * 128:(m + 1) * 128], on[:, c, :], start=(c == 0), stop=(c == 7)),
                                   r=["c_wxo", "c_on"], w=[PS(bk)])
                            op("dve", lambda e, m=m, bk=bk, sl=sl: e.tensor_tensor(out=xb[sl][:, m, :], in0=xb[sl][:, m, :], in1=ps[:, bk, :], op=ALU.add), r=[xk, PS(bk)], w=[xk])
                        dma("cs%d" % sl, lambda q, sl=sl, nm=nm, t0=t0: q.dma_start(out=X1[nm][:, :, t0:t0 + 512], in_=xb[sl][:]), r=[xk], w=xkeys("X1", nm, t0, t0 + 512))
                mk.barrier()

            with ExitStack() as st:
                wgu = sb("f_wgu", [128, 8, 2 * DFF], BF16, st)
                load_w(wgu, "f_wgu", W["ffn_wgu"][L], 1024, 2 * DFF, gain=("ffn_norm", L))
                NH = NFC // 2
                wdn = [sb("f_wdn%d" % i, [128, NH, 128], BF16, st) for i in range(2)]
                xb = [sb("f_xb%d" % i, [128, 8, 512], F32, st) for i in range(2)]
                hT = sb("f_hT", [128, 8, 512], BF16, st)
                rstd = sb("f_rstd", [128, 512], F32, st)
                rtmp = sb("f_rtmp", [128, 512], F32, st)
                acc = [sb("f_acc%d" % i, [128, 512], F32, st) for i in range(3)]
                sg = [sb("f_sg%d" % i, [128, 512], F32, st) for i in range(3)]
                mT = sb("f_mT", [128, NH, 512], BF16, st)
                wc = 0
                for half in range(2):
                    for m in range(8):
                        wsl = wcnt[0] % 2
                        wcnt[0] += 1
                        ws = wc % 2
                        wc += 1
                        dma("w%d" % wsl, lambda q, wsl=wsl, m=m, half=half: q.dma_start(out=wst[wsl][:, 0:NH * 128].rearrange("p (c n) -> p c n", c=NH),
                                                                                      in_=W["ffn_wdn"][L][half * NH * 128:(half + 1) * NH * 128, m * 128:(m + 1) * 128].rearrange("(c p) n -> p c n", p=128)),
                            w=[("wst", wsl)])
                        op("dve" if (wc % 2) else "pool", lambda e, ws=ws, wsl=wsl: e.tensor_copy(out=wdn[ws][:].rearrange("p c n -> p (c n)"), in_=wst[wsl][:, 0:NH * 128]), r=[("wst", wsl)], w=[("f_wdn", ws)])
                        dma("fwb%d" % ws, lambda q, ws=ws, m=m, half=half: q.dma_start(out=WDB[half, m], in_=wdn[ws][:]), r=[("f_wdn", ws)], w=[("WDB", half, m)])
                fblocks = []
                for nm, S in seqs:
                    nblk = -(-S // 510)
                    wout = -(-S // nblk)
                    wout = -(-wout // 2) * 2
                    for b in range(nblk):
                        a0 = b * wout
                        a1 = min(S, a0 + wout)
                        l0 = max(0, a0 - 1)
                        l1 = min(S, a1 + 1)
                        padl = 1 if a0 == 0 else 0
                        padr = 1 if a1 == S else 0
                        n = (l1 - l0) + padl + padr
                        no = a1 - a0
                        assert n == no + 2 and n <= 512
                        fblocks.append((nm, S, a0, a1, l0, l1, padl, padr, n, no))

                def f_load(bi):
                    nm, S, a0, a1, l0, l1, padl, padr, n, no = fblocks[bi]
                    sl = bi % 2
                    xk = ("f_xb", sl)
                    if padl:
                        op("pool", lambda e: e.memset(xb[sl][:, :, 0:1], 0.0), w=[xk])
                    if padr:
                        op("pool", lambda e: e.memset(xb[sl][:, :, n - 1:n], 0.0), w=[xk])
                    dma("fx%d" % sl, lambda q: q.dma_start(out=xb[sl][:, :, padl:padl + l1 - l0], in_=X1[nm][:, :, l0:l1]),
                        r=xkeys("X1", nm, l0, l1), w=[xk])

                f_load(0)
                for bi, (nm, S, a0, a1, l0, l1, padl, padr, n, no) in enumerate(fblocks):
                    if True:
                        if bi + 1 < len(fblocks):
                            f_load(bi + 1)
                        sl = bi % 2
                        xk = ("f_xb", sl)
                        norm_block(xb[sl], xk, hT, "f_hT", n, rtmp, rstd, "f")
                        for half in range(2):
                            for fi in range(NH):
                                fc = half * NH + fi
                                ab = fc % 3
                                gb, ub = 2 * ab, 2 * ab + 1
                                for c in range(8):
                                    op("pe", lambda e, c=c, fc=fc, gb=gb, n=n: e.matmul(ps[:, gb, 0:n], wgu[:, c, fc * 128:(fc + 1) * 128], hT[:, c, 0:n], start=(c == 0), stop=(c == 7)),
                                       r=["f_wgu", "f_hT"], w=[PS(gb)])
                                for c in range(8):
                                    op("pe", lambda e, c=c, fc=fc, ub=ub, no=no: e.matmul(ps[:, ub, 0:no], wgu[:, c, DFF + fc * 128:DFF + (fc + 1) * 128], hT[:, c, 1:1 + no], start=(c == 0), stop=(c == 7)),
                                       r=["f_wgu", "f_hT"], w=[PS(ub)])
                                cw0 = gcol(("cw", L), 0 * NFC + fc); cw1 = gcol(("cw", L), 1 * NFC + fc); cw2 = gcol(("cw", L), 2 * NFC + fc)
                                cbv = gcol(("cb", L), fc)
                                op("act", lambda e, gb=gb, ab=ab, no=no, cw1=cw1, cbv=cbv: e.activation(out=acc[ab][:, 0:no], in_=ps[:, gb, 1:1 + no], func=AF.Identity, scale=cw1, bias=cbv),
                                   r=[PS(gb), "gv"], w=[("f_acc", ab)])
                                op("dve", lambda e, gb=gb, ab=ab, no=no, cw0=cw0: e.scalar_tensor_tensor(out=acc[ab][:, 0:no], in0=ps[:, gb, 0:no], scalar=cw0, in1=acc[ab][:, 0:no], op0=ALU.mult, op1=ALU.add),
                                   r=[PS(gb), "gv", ("f_acc", ab)], w=[("f_acc", ab)])
                                op("dve", lambda e, gb=gb, ab=ab, no=no, cw2=cw2: e.scalar_tensor_tensor(out=acc[ab][:, 0:no], in0=ps[:, gb, 2:2 + no], scalar=cw2, in1=acc[ab][:, 0:no], op0=ALU.mult, op1=ALU.add),
                                   r=[PS(gb), "gv", ("f_acc", ab)], w=[("f_acc", ab)])
                                op("act", lambda e, ab=ab, no=no: e.activation(out=sg[ab][:, 0:no], in_=acc[ab][:, 0:no], func=AF.Silu), r=[("f_acc", ab)], w=[("f_sg", ab)])
                                op("dve", lambda e, ab=ab, ub=ub, no=no, fi=fi: e.tensor_tensor(out=mT[:, fi, 0:no], in0=sg[ab][:, 0:no], in1=ps[:, ub, 0:no], op=ALU.mult), r=[("f_sg", ab), PS(ub)], w=["f_mT"])
                            for m in range(8):
                                ws = wc % 2
                                wc += 1
                                dma("fw%d" % ws, lambda q, ws=ws, m=m, half=half: q.dma_start(out=wdn[ws][:], in_=WDB[half, m]), r=[("WDB", half, m)], w=[("f_wdn", ws)])
                                bk = 6 + (m % 2)
                                for fi in range(NH):
                                    op("pe", lambda e, fi=fi, ws=ws, bk=bk, no=no: e.matmul(ps[:, bk, 0:no], wdn[ws][:, fi, :], mT[:, fi, 0:no], start=(fi == 0), stop=(fi == NH - 1)),
                                       r=[("f_wdn", ws), "f_mT"], w=[PS(bk)])
                                op("dve", lambda e, m=m, bk=bk, sl=sl, no=no: e.tensor_tensor(out=xb[sl][:, m, 1:1 + no], in0=xb[sl][:, m, 1:1 + no], in1=ps[:, bk, 0:no], op=ALU.add), r=[xk, PS(bk)], w=[xk])
                        dma("fs%d" % sl, lambda q, sl=sl, nm=nm, a0=a0, a1=a1, no=no: q.dma_start(out=X0[nm][:, :, a0:a1], in_=xb[sl][:, :, 1:1 + no]), r=[xk], w=xkeys("X0", nm, a0, a1))
                mk.barrier()

        with ExitStack() as st:
            tin = [sb("o_in%d" % i, [128, 8, 128], F32, st) for i in range(2)]
            tout = [sb("o_out%d" % i, [128, 1024], F32, st) for i in range(2)]
            it = 0
            for nm, S in seqs:
                for tt in range(S // 128):
                    sl = it % 2
                    it += 1
                    dma("oi%d" % sl, lambda q, sl=sl, tt=tt, nm=nm: q.dma_start(out=tin[sl][:], in_=X0[nm][:, :, tt * 128:(tt + 1) * 128]), r=[("X0", nm, tt)], w=[("o_in", sl)])
                    for c in range(8):
                        bk = (c // 4) + 2 * sl
                        op("pe", lambda e, c=c, bk=bk, sl=sl: e.transpose(ps[:, bk, (c % 4) * 128:(c % 4 + 1) * 128], tin[sl][:, c, :], ident[:]), r=[("o_in", sl), "ident"], w=[PS(bk)])
                    op("dve", lambda e, sl=sl: e.tensor_copy(out=tout[sl][:, 0:512], in_=ps[:, 2 * sl, :]), r=[PS(2 * sl)], w=[("o_out", sl)])
                    op("act", lambda e, sl=sl: e.activation(out=tout[sl][:, 512:1024], in_=ps[:, 2 * sl + 1, :], func=AF.Copy), r=[PS(2 * sl + 1)], w=[("o_out", sl)])
                    dma("oo%d" % sl, lambda q, sl=sl, tt=tt, nm=nm: q.dma_start(out=yout[nm][tt * 128:(tt + 1) * 128, :], in_=tout[sl][:]), r=[("o_out", sl)], w=[("Y", nm, tt)])
        mk.finish()
        nc._mk_nops = mk.nops
    return nc


_CFG = dict(SP=8192, SS=2048, NS=2, depth=4, ncores=8)


def kernel(**inputs):
    cfg = _CFG
    SP, SS, NS, depth, ncores = cfg["SP"], cfg["SS"], cfg["NS"], cfg["depth"], cfg["ncores"]
    inp = {k: np.asarray(v) for k, v in inputs.items()}
    nc = build_program(SP, SS, NS, depth)
    prep = _host_prep(inp, depth)
    consts = _const_tables(max(SP, SS))
    in_maps = []
    for c in range(ncores):
        m = {}
        if SP:
            m["x_p"] = np.ascontiguousarray(inp["x_prompt"][c])
            m["mem_p"] = np.ascontiguousarray(inp["mem_prompt"][c])
        for i in range(NS):
            m["x_s%d" % i] = np.ascontiguousarray(inp["x_sample"][c * NS + i])
            m["mem_s%d" % i] = np.ascontiguousarray(inp["mem_sample"][c * NS + i])
        m.update(prep)
        m.update(consts)
        in_maps.append(m)
    res = run_bass_kernel_spmd(nc, in_maps, core_ids=list(range(ncores)))
    outs = []
    if SP:
        outs.append(np.stack([res.results[c]["y_p"] for c in range(ncores)], axis=0).astype(np.float32))
    ys = np.stack([res.results[c]["y_s%d" % i] for c in range(ncores) for i in range(NS)], axis=0).astype(np.float32)
    outs.append(ys)
    return tuple(outs)
```

```python
import math
from contextlib import ExitStack
import numpy as np
import concourse.bass as bass
import concourse.mybir as mybir
from concourse.bass_utils import run_bass_kernel_spmd

F32 = mybir.dt.float32
BF16 = mybir.dt.bfloat16
AF = mybir.ActivationFunctionType
ALU = mybir.AluOpType

D = 1024
NMEM = 256
EPS = 1e-6
DFF = 2816
NFC = DFF // 128
ROPE_THETA = 10000.0


class MK:
    def __init__(self, nc, es):
        self.nc = nc
        self.es = es
        self.engs = {"pe": nc.tensor, "act": nc.scalar, "dve": nc.vector, "pool": nc.gpsimd, "sp": nc.sync}
        self.sem = {}
        self.cnt = {}
        for e in self.engs:
            self.sem[e] = es.enter_context(nc.semaphore("s_" + e))
            self.cnt[e] = 0
        self.known = {e: {} for e in self.engs}
        self.lastw = {}
        self.readers = {}
        self.nops = 0

    def _chan(self, ch):
        if ch not in self.sem:
            self.sem[ch] = self.es.enter_context(self.nc.semaphore("d_" + ch))
            self.cnt[ch] = 0
        return self.sem[ch]

    def _deps(self, e, r, w, extra=()):
        deps = {}
        def add(t):
            s, c, de = t
            if de == "pe" and e == "pe":
                return
            if deps.get(s, 0) < c:
                deps[s] = c
        for k in list(r) + list(w):
            if k in self.lastw:
                add(self.lastw[k])
        for k in w:
            for s, (c, de) in self.readers.get(k, {}).items():
                add((s, c, de))
        for t in extra:
            add(t)
        eng = self.engs[e]
        kn = self.known[e]
        for s, c in deps.items():
            if kn.get(s, 0) >= c:
                continue
            eng.wait_ge(self.sem[s], c)
            kn[s] = c
            self.nops += 1

    def _record(self, me, r, w):
        s, c, de = me
        for k in w:
            self.lastw[k] = me
            self.readers[k] = {}
        for k in r:
            self.readers.setdefault(k, {})[s] = (c, de)

    def op(self, e, fn, r=(), w=()):
        self._deps(e, r, w)
        ins = fn(self.engs[e])
        self.cnt[e] += 1
        ins.then_inc(self.sem[e], 1)
        self.nops += 1
        self._record((e, self.cnt[e], e), r, w)

    def dma(self, ch, fns, r=(), w=()):
        sem = self._chan(ch)
        extra = [(ch, self.cnt[ch], "dma")] if self.cnt[ch] else []
        self._deps("sp", r, w, extra)
        if not isinstance(fns, (list, tuple)):
            fns = [fns]
        for fn in fns:
            ins = fn(self.nc.sync)
            ins.then_inc(sem, 16)
            self.cnt[ch] += 16
            self.nops += 1
        self._record((ch, self.cnt[ch], "dma"), r, w)

    def barrier(self):
        for e in self.engs:
            eng = self.engs[e]
            kn = self.known[e]
            for s, c in self.cnt.items():
                if c and kn.get(s, 0) < c and not (s == e == "pe"):
                    eng.wait_ge(self.sem[s], c)
                    kn[s] = c
                    self.nops += 1

    def finish(self):
        kn = self.known["sp"]
        for s, c in self.cnt.items():
            if c and kn.get(s, 0) < c:
                self.nc.sync.wait_ge(self.sem[s], c)
                kn[s] = c


def _gpack_layout(depth):
    off = {}
    n = 0
    nA = (depth + 1) // 2
    nB = depth // 2
    def add(name, w):
        nonlocal n
        off[name] = n
        n += w
    for j in range(nA):
        add(("mla_norm", j), 8); add(("qlat", j), 3); add(("kvlat", j), 2)
        add(("g1q", j), 1); add(("g2q", j), 1); add(("g1k", j), 1); add(("g2k", j), 1)
    for j in range(nB):
        add(("diff_norm", j), 8); add(("dq", j), 1); add(("dk", j), 1); add(("dsub", j), 1)
    for i in range(depth):
        add(("xa_norm", i), 8); add(("xa_mem", i), 8); add(("xq", i), 2); add(("xk", i), 2)
        add(("ffn_norm", i), 8); add(("cw", i), 3 * NFC); add(("cb", i), NFC)
    return off, n


def _cols(v, nchunk):
    return np.ascontiguousarray(np.asarray(v, np.float32).reshape(nchunk, 128).T)


def _host_prep(inp, depth):
    nA = (depth + 1) // 2
    nB = depth // 2
    off, ncol = _gpack_layout(depth)
    gp = np.zeros((128, ncol), np.float32)
    out = {}
    pr = (np.arange(32) + 16) % 32
    if nA:
        wd = inp["mla_w_down"][:nA]
        out["mla_wd"] = np.ascontiguousarray(np.concatenate([wd[:, :, :640], wd[:, :, 640:672], wd[:, :, 640 + pr]], axis=2))
        wuq = inp["mla_w_uq"][:nA].reshape(nA, 384, 8, 96)
        q1 = np.zeros((nA, 384, 8, 128), np.float32)
        q1[..., 0:32] = wuq[..., 64:96]
        q1[..., 64:128] = wuq[..., 0:64]
        out["mla_wq1"] = q1.reshape(nA, 384, 1024)
        out["mla_wq2"] = np.ascontiguousarray(wuq[..., 64 + pr]).reshape(nA, 384, 256)
        wkv = inp["mla_w_ukv"][:nA].reshape(nA, 256, 8, 128)
        kn = np.zeros((nA, 256, 8, 128), np.float32)
        kn[..., 64:128] = wkv[..., 0:64]
        out["mla_wkn"] = kn.reshape(nA, 256, 1024)
        out["mla_wv"] = np.ascontiguousarray(wkv[..., 64:128]).reshape(nA, 256, 512)
        out["mla_wo"] = np.ascontiguousarray(inp["mla_w_o"][:nA])
    for j in range(nA):
        gp[:, off[("mla_norm", j)]:][:, :8] = _cols(inp["mla_norm"][j], 8)
        gp[:, off[("qlat", j)]:][:, :3] = _cols(inp["mla_q_lat_norm"][j], 3)
        gp[:, off[("kvlat", j)]:][:, :2] = _cols(inp["mla_kv_lat_norm"][j], 2)
        for nm, g in (("q", inp["mla_q_norm"][j]), ("k", inp["mla_k_norm"][j])):
            c1 = off[("g1" + nm, j)]
            gp[0:32, c1] = g[64:96]
            gp[64:128, c1] = g[0:64]
            gp[0:32, off[("g2" + nm, j)]] = g[64 + pr]
    if nB:
        out["diff_wqkv"] = np.ascontiguousarray(inp["diff_w_qkv"][:nB])
        out["diff_wo"] = np.ascontiguousarray(inp["diff_w_o"][:nB])
        out["diff_lam"] = np.ascontiguousarray(inp["diff_lambda"][:nB].reshape(nB, 1, 256))
    for j in range(nB):
        gp[:, off[("diff_norm", j)]:][:, :8] = _cols(inp["diff_norm"][j], 8)
        gp[:, off[("dq", j)]] = inp["diff_q_norm"][j].reshape(128)
        gp[:, off[("dk", j)]] = inp["diff_k_norm"][j].reshape(128)
        gp[:, off[("dsub", j)]] = inp["diff_sub_norm"][j]
    for i in range(depth):
        gp[:, off[("xa_norm", i)]:][:, :8] = _cols(inp["xa_norm"][i], 8)
        gp[:, off[("xa_mem", i)]:][:, :8] = _cols(inp["xa_mem_norm"][i], 8)
        gp[:, off[("xq", i)]:][:, :2] = _cols(inp["xa_q_norm"][i], 2)
        gp[:, off[("xk", i)]:][:, :2] = _cols(inp["xa_k_norm"][i], 2)
        gp[:, off[("ffn_norm", i)]:][:, :8] = _cols(inp["ffn_norm"][i], 8)
        cw = inp["ffn_conv_w"][i]
        for t in range(3):
            gp[:, off[("cw", i)] + t * NFC:][:, :NFC] = _cols(cw[t], NFC)
        gp[:, off[("cb", i)]:][:, :NFC] = _cols(inp["ffn_conv_b"][i], NFC)
    out["xa_wq"] = np.ascontiguousarray(inp["xa_w_q"][:depth])
    out["xa_wkv"] = np.ascontiguousarray(inp["xa_w_kv"][:depth])
    out["xa_wo"] = np.ascontiguousarray(inp["xa_w_o"][:depth])
    out["ffn_wgu"] = np.ascontiguousarray(inp["ffn_w_gu"][:depth])
    out["ffn_wdn"] = np.ascontiguousarray(inp["ffn_w_down"][:depth])
    out["gpack"] = gp
    return out


def _const_tables(smax):
    c = {}
    c["ident"] = np.eye(128, dtype=np.float32)
    inv = ROPE_THETA ** (-np.arange(0, 32, 2, dtype=np.float32) / 32.0)
    ang = np.arange(smax, dtype=np.float32)[:, None] * inv[None, :]
    ang = np.concatenate([ang, ang], axis=-1).astype(np.float32)
    cos = np.cos(ang).T
    sin = np.sin(ang).T.copy()
    sin[0:16] *= -1.0
    c["ropecos"] = np.ascontiguousarray(cos, dtype=np.float32)
    c["ropessin"] = np.ascontiguousarray(sin, dtype=np.float32)
    import ml_dtypes
    pos = np.arange(smax)
    qa = np.zeros((8, 512), np.float32)
    q = np.arange(512)
    qa[0] = q % 128; qa[1] = 128 * (q // 128); qa[2] = 1; qa[3] = 1
    qa[4:8] = -qa[0:4]
    c["alibi_q"] = qa.astype(ml_dtypes.bfloat16)
    ka = np.zeros((8, 4, smax), np.float32)
    for h in range(8):
        m = 2.0 ** (-(h + 1)) * 8.0
        ka[h, 0] = -m; ka[h, 1] = -m
        ka[h, 2] = m * (pos % 128); ka[h, 3] = m * 128 * ((pos // 128) % 4)
    c["alibi_k"] = ka.astype(ml_dtypes.bfloat16)
    kk = np.arange(512)
    dist = np.abs(q[None, :] - kk[:, None]).astype(np.float32)
    c["alibi_diag"] = np.ascontiguousarray(dist.reshape(4, 128, 512).transpose(1, 0, 2))
    return c


def build_program(SP, SS, NS, depth):
    nc = bass.Bass("TRN2", target_bir_lowering=False)
    nA = (depth + 1) // 2
    nB = depth // 2
    goff, gcols = _gpack_layout(depth)
    seqs = []
    if SP:
        seqs.append(("p", SP))
    for i in range(NS):
        seqs.append(("s%d" % i, SS))
    smax = max(s for _, s in seqs)

    def din(name, shape, dt=F32):
        return nc.dram_tensor(name, list(shape), dt, kind="ExternalInput").ap()

    def dscr(name, shape, dt):
        return nc.dram_tensor(name, list(shape), dt).ap()

    xin, memin, yout = {}, {}, {}
    for nm, S in seqs:
        xin[nm] = din("x_" + nm, [S, D])
        memin[nm] = din("mem_" + nm, [NMEM, D])
        yout[nm] = nc.dram_tensor("y_" + nm, [S, D], F32, kind="ExternalOutput").ap()
    W = {}
    if nA:
        W["mla_wd"] = din("mla_wd", [nA, 1024, 704]); W["mla_wq1"] = din("mla_wq1", [nA, 384, 1024])
        W["mla_wq2"] = din("mla_wq2", [nA, 384, 256]); W["mla_wkn"] = din("mla_wkn", [nA, 256, 1024])
        W["mla_wv"] = din("mla_wv", [nA, 256, 512]); W["mla_wo"] = din("mla_wo", [nA, 512, 1024])
    if nB:
        W["diff_wqkv"] = din("diff_wqkv", [nB, 1024, 3072]); W["diff_wo"] = din("diff_wo", [nB, 1024, 1024])
        W["diff_lam"] = din("diff_lam", [nB, 1, 256])
    W["xa_wq"] = din("xa_wq", [depth, 1024, 1024]); W["xa_wkv"] = din("xa_wkv", [depth, 1024, 2048])
    W["xa_wo"] = din("xa_wo", [depth, 1024, 1024])
    W["ffn_wgu"] = din("ffn_wgu", [depth, 1024, 2 * DFF]); W["ffn_wdn"] = din("ffn_wdn", [depth, DFF, 1024])
    gpack_d = din("gpack", [128, gcols])
    ident_d = din("ident", [128, 128])
    cos_d = din("ropecos", [32, smax]); ssin_d = din("ropessin", [32, smax])
    alq_d = din("alibi_q", [8, 512], BF16); alk_d = din("alibi_k", [8, 4, smax], BF16)
    ald_d = din("alibi_diag", [128, 4, 512])

    X0, X1, QT, KT, VD, OT, MEMK, MEMV = {}, {}, {}, {}, {}, {}, {}, {}
    for nm, S in seqs:
        X0[nm] = dscr("X0_" + nm, [128, 8, S], F32)
        X1[nm] = dscr("X1_" + nm, [128, 8, S], F32)
        QT[nm] = dscr("QT_" + nm, [8, 128, S], BF16)
        KT[nm] = dscr("KT_" + nm, [8, 128, S], BF16)
        VD[nm] = dscr("VD_" + nm, [8, 128, S // 128, 130], BF16)
        OT[nm] = dscr("OT_" + nm, [8, 128, S], BF16)
        MEMK[nm] = dscr("MEMN_" + nm, [128, 8, NMEM], BF16)
    WDB = dscr("WDB", [2, 8, 128, NFC // 2, 128], BF16)

    es = ExitStack()
    with es:
        mk = MK(nc, es)
        op, dma = mk.op, mk.dma

        sbn = [0]

        def sb(name, shape, dt=F32, stack=es):
            sbn[0] += 1
            return stack.enter_context(nc.sbuf_tensor("sb%d_%s" % (sbn[0], name), list(shape), dt))

        ps = es.enter_context(nc.psum_tensor("ps", [128, 8, 512], F32))

        def PS(b):
            return ("ps", b)

        gv = sb("gv", [128, gcols])
        ident = sb("ident", [128, 128])
        ones_bf = sb("ones_bf", [128, 128], BF16)
        ones_f = sb("ones_f", [128, 128])
        ones2 = sb("ones2", [128, 128], BF16)
        epsc = sb("epsc", [128, 1])
        dma("c0", lambda q: q.dma_start(out=gv[:], in_=gpack_d[:, :]), w=["gv"])
        dma("c1", lambda q: q.dma_start(out=ident[:], in_=ident_d[:, :]), w=["ident"])
        op("pool", lambda e: e.memset(ones_bf[:], 1.0), w=["ones_bf"])
        op("pool", lambda e: e.memset(ones_f[:], 1.0), w=["ones_f"])
        op("pool", lambda e: e.memset(ones2[:], 0.0), w=["ones2"])
        op("pool", lambda e: e.memset(ones2[0:64, 0:64], 1.0), r=["ones2"], w=["ones2"])
        op("pool", lambda e: e.memset(ones2[64:128, 64:128], 1.0), r=["ones2"], w=["ones2"])
        op("pool", lambda e: e.memset(epsc[:], EPS), w=["epsc"])

        def gcol(key, c=0, rows=slice(0, 128)):
            o = goff[key] + c
            return gv[rows, o:o + 1]

        def rstd_from_ps(bank, n, dim, out_ap, key_out, rows=slice(0, 128), tmp=None, tmpkey=None):
            op("act", lambda e: e.activation(out=tmp[rows, 0:n], in_=ps[rows, bank, 0:n], func=AF.Ln, scale=1.0 / dim, bias=epsc[rows, 0:1]),
               r=[PS(bank), "epsc"], w=[tmpkey])
            op("act", lambda e: e.activation(out=out_ap, in_=tmp[rows, 0:n], func=AF.Exp, scale=-0.5), r=[tmpkey], w=[key_out])

        WSTW = 1536
        wst = [sb("wst%d" % i, [128, WSTW]) for i in range(2)]
        wcnt = [0]

        def load_w(dst, dkey, src, K, N, gain=None, scale=1.0):
            first = True
            for kc in range(K // 128):
                for n0 in range(0, N, WSTW):
                    n1 = min(N, n0 + WSTW)
                    sl = wcnt[0] % 2
                    wcnt[0] += 1
                    st = wst[sl]
                    dma("w%d" % sl, lambda q, st=st, kc=kc, n0=n0, n1=n1: q.dma_start(out=st[:, 0:n1 - n0], in_=src[kc * 128:(kc + 1) * 128, n0:n1]),
                        w=[("wst", sl)])
                    e = "dve" if (wcnt[0] % 2) else "act"
                    if gain is not None:
                        g = gcol(gain, kc)
                        if e == "dve":
                            op(e, lambda en, st=st, kc=kc, n0=n0, n1=n1, g=g: en.tensor_scalar(out=dst[:, kc, n0:n1], in0=st[:, 0:n1 - n0], scalar1=g, scalar2=None, op0=ALU.mult),
                               r=[("wst", sl), "gv"], w=[dkey])
                        else:
                            op(e, lambda en, st=st, kc=kc, n0=n0, n1=n1, g=g: en.activation(out=dst[:, kc, n0:n1], in_=st[:, 0:n1 - n0], func=AF.Copy, scale=g),
                               r=[("wst", sl), "gv"], w=[dkey])
                    else:
                        if e == "dve":
                            op(e, lambda en, st=st, kc=kc, n0=n0, n1=n1: en.tensor_copy(out=dst[:, kc, n0:n1], in_=st[:, 0:n1 - n0]),
                               r=[("wst", sl)], w=[dkey])
                        else:
                            op(e, lambda en, st=st, kc=kc, n0=n0, n1=n1: en.activation(out=dst[:, kc, n0:n1], in_=st[:, 0:n1 - n0], func=AF.Copy),
                               r=[("wst", sl)], w=[dkey])
                    first = False

        def xkeys(tag, nm, t0, t1):
            return [(tag, nm, i) for i in range(t0 // 128, (t1 - 1) // 128 + 1)]

        def norm_block(xb, xkey, hT, hkey, n, rtmp, rstd, sfx):
            op("dve", lambda e: e.tensor_tensor(out=hT[:, :, 0:n], in0=xb[:, :, 0:n], in1=xb[:, :, 0:n], op=ALU.mult), r=[xkey], w=[hkey])
            for c in range(8):
                op("pe", lambda e, c=c: e.matmul(ps[:, 7, 0:n], ones_bf[:], hT[:, c, 0:n], start=(c == 0), stop=(c == 7)), r=[hkey, "ones_bf"], w=[PS(7)])
            rstd_from_ps(7, n, 1024.0, rstd[:, 0:n], "rstd" + sfx, tmp=rtmp, tmpkey="rtmp" + sfx)
            op("dve", lambda e: e.tensor_tensor(out=hT[:, :, 0:n], in0=xb[:, :, 0:n], in1=rstd[:, 0:n].unsqueeze(1).to_broadcast([128, 8, n]), op=ALU.mult),
               r=[xkey, "rstd" + sfx, PS(7)], w=[hkey])

        with ExitStack() as st:
            tin = [sb("tin%d" % i, [128, 1024], F32, st) for i in range(2)]
            tout = [sb("tout%d" % i, [128, 8, 128], F32, st) for i in range(2)]
            msq = sb("msq", [128, 1024], F32, st)
            mtmp = [sb("mtmp%d" % i, [128, 8, 128], BF16, st) for i in range(2)]
            mss = sb("mss", [128, 1], F32, st)
            mrs = sb("mrs", [128, 1], F32, st)
            it = 0
            for nm, S in seqs:
                for tt in range(S // 128):
                    sl = it % 2
                    it += 1
                    dma("ti%d" % sl, lambda q, sl=sl, tt=tt, nm=nm: q.dma_start(out=tin[sl][:], in_=xin[nm][tt * 128:(tt + 1) * 128, :]), w=[("tin", sl)])
                    for c in range(8):
                        bk = (c // 4) + 2 * sl
                        op("pe", lambda e, c=c, bk=bk, sl=sl: e.transpose(ps[:, bk, (c % 4) * 128:(c % 4 + 1) * 128], tin[sl][:, c * 128:(c + 1) * 128], ident[:]),
                           r=[("tin", sl), "ident"], w=[PS(bk)])
                    for hh in range(2):
                        bk = hh + 2 * sl
                        eng = "dve" if hh == 0 else "act"
                        if eng == "dve":
                            op("dve", lambda e, bk=bk, hh=hh, sl=sl: e.tensor_copy(out=tout[sl][:, hh * 4:(hh + 1) * 4, :], in_=ps[:, bk, :].rearrange("p (c t) -> p c t", c=4)),
                               r=[PS(bk)], w=[("tout", sl)])
                        else:
                            op("act", lambda e, bk=bk, hh=hh, sl=sl: e.activation(out=tout[sl][:, hh * 4:(hh + 1) * 4, :], in_=ps[:, bk, :].rearrange("p (c t) -> p c t", c=4), func=AF.Copy),
                               r=[PS(bk)], w=[("tout", sl)])
                    dma("to%d" % sl, lambda q, sl=sl, tt=tt, nm=nm: q.dma_start(out=X0[nm][:, :, tt * 128:(tt + 1) * 128], in_=tout[sl][:]),
                        r=[("tout", sl)], w=[("X0", nm, tt)])
                for mt in range(2):
                    sl = it % 2
                    it += 1
                    dma("ti%d" % sl, lambda q, sl=sl, mt=mt, nm=nm: q.dma_start(out=tin[sl][:], in_=memin[nm][mt * 128:(mt + 1) * 128, :]), w=[("tin", sl)])
                    op("dve", lambda e, sl=sl: e.tensor_tensor(out=msq[:], in0=tin[sl][:], in1=tin[sl][:], op=ALU.mult), r=[("tin", sl)], w=["msq"])
                    op("dve", lambda e: e.tensor_reduce(out=mss[:], in_=msq[:], axis=mybir.AxisListType.X, op=ALU.add), r=["msq"], w=["mss"])
                    op("act", lambda e: e.activation(out=mrs[:], in_=mss[:], func=AF.Ln, scale=1.0 / 1024.0, bias=epsc[:, 0:1]), r=["mss", "epsc"], w=["mrs"])
                    op("act", lambda e: e.activation(out=mss[:], in_=mrs[:], func=AF.Exp, scale=-0.5), r=["mrs"], w=["mss"])
                    op("dve", lambda e, sl=sl: e.tensor_scalar(out=msq[:], in0=tin[sl][:], scalar1=mss[:, 0:1], scalar2=None, op0=ALU.mult), r=[("tin", sl), "mss"], w=["msq"])
                    for c in range(8):
                        bk = (c // 4) + 2 * sl
                        op("pe", lambda e, c=c, bk=bk: e.transpose(ps[:, bk, (c % 4) * 128:(c % 4 + 1) * 128], msq[:, c * 128:(c + 1) * 128], ident[:]),
                           r=["msq", "ident"], w=[PS(bk)])
                    for hh in range(2):
                        bk = hh + 2 * sl
                        op("dve", lambda e, bk=bk, hh=hh, sl=sl: e.tensor_copy(out=mtmp[sl][:, hh * 4:(hh + 1) * 4, :], in_=ps[:, bk, :].rearrange("p (c t) -> p c t", c=4)),
                           r=[PS(bk)], w=[("mtmp", sl)])
                    dma("tm%d" % sl, lambda q, sl=sl, mt=mt, nm=nm: q.dma_start(out=MEMK[nm][:, :, mt * 128:(mt + 1) * 128], in_=mtmp[sl][:]), r=[("mtmp", sl)], w=[("memn", nm)])
            mk.barrier()

        for L in range(depth):
            j = L // 2
            is_mla = (L % 2 == 0)
            with ExitStack() as st:
                xb = [sb("a_xb%d" % i, [128, 8, 512], F32, st) for i in range(2)]
                hT = sb("a_hT", [128, 8, 512], BF16, st)
                rtmp = sb("a_rtmp", [128, 512], F32, st)
                rstd = sb("a_rstd", [128, 512], F32, st)
                if is_mla:
                    wd = sb("a_wd", [128, 8, 704], BF16, st)
                    wq1 = sb("a_wq1", [128, 3, 1024], BF16, st)
                    wq2 = sb("a_wq2", [128, 3, 256], BF16, st)
                    wkn = sb("a_wkn", [128, 2, 1024], BF16, st)
                    wv = sb("a_wv", [128, 2, 512], BF16, st)
                    load_w(wd, "a_wd", W["mla_wd"][j], 1024, 704, gain=("mla_norm", j))
                    load_w(wq1, "a_wq1", W["mla_wq1"][j], 384, 1024, gain=("qlat", j))
                    load_w(wq2, "a_wq2", W["mla_wq2"][j], 384, 256, gain=("qlat", j))
                    load_w(wkn, "a_wkn", W["mla_wkn"][j], 256, 1024, gain=("kvlat", j))
                    load_w(wv, "a_wv", W["mla_wv"][j], 256, 512, gain=("kvlat", j))
                    lsq = sb("a_lsq", [128, 5, 512], BF16, st)
                    lrs = sb("a_lrs", [128, 2, 512], F32, st)
                    cqn = sb("a_cqn", [128, 3, 512], BF16, st)
                    ckvn = sb("a_ckvn", [128, 2, 512], BF16, st)
                    cosbs = [sb("a_cos%d" % i, [32, 512], F32, st) for i in range(2)]
                    sinbs = [sb("a_sin%d" % i, [32, 512], F32, st) for i in range(2)]
                    tA = sb("a_tA", [32, 512], F32, st)
                    tB = sb("a_tB", [32, 512], F32, st)
                    tK = sb("a_tK", [32, 512], F32, st)
                    sqh = [sb("a_sqh%d" % i, [128, 512], BF16, st) for i in range(2)]
                    hr = [sb("a_hr%d" % i, [128, 512], F32, st) for i in range(2)]
                    qo = [sb("a_qo%d" % i, [128, 512], BF16, st) for i in range(4)]
                    vsb = [sb("a_v%d" % i, [128, 8, 65], BF16, st) for i in range(2)]
                    for i in range(2):
                        op("pool", lambda e, i=i: e.memset(sqh[i][:], 0.0), w=[("sqh", i)])
                        op("pool", lambda e, i=i: e.memset(vsb[i][:], 1.0), w=[("vsb", i)])
                    for i in range(4):
                        op("pool", lambda e, i=i: e.memset(qo[i][:], 0.0), w=[("qo", i)])
                else:
                    wqkv = sb("a_wqkv", [128, 8, 3072], BF16, st)
                    load_w(wqkv, "a_wqkv", W["diff_wqkv"][j], 1024, 3072, gain=("diff_norm", j))
                    sqh = [sb("a_sqh%d" % i, [128, 512], BF16, st) for i in range(4)]
                    hr = [sb("a_hr%d" % i, [128, 512], F32, st) for i in range(4)]
                    qo = [sb("a_qo%d" % i, [128, 512], BF16, st) for i in range(4)]
                    vsb = [sb("a_v%d" % i, [128, 1024], BF16, st) for i in range(2)]
                qoc = 0
                vc = 0
                ablocks = [(nm, S, b) for nm, S in seqs for b in range(S // 512)]

                def a_load(bi):
                    nm, S, b = ablocks[bi]
                    sl = bi % 2
                    t0 = b * 512
                    dma("ax%d" % sl, lambda q: q.dma_start(out=xb[sl][:], in_=X0[nm][:, :, t0:t0 + 512]),
                        r=xkeys("X0", nm, t0, t0 + 512), w=[("a_xb", sl)])
                    if is_mla:
                        dma("acs%d" % sl, [lambda q: q.dma_start(out=cosbs[sl][:], in_=cos_d[:, t0:t0 + 512]),
                                           lambda q: q.dma_start(out=sinbs[sl][:], in_=ssin_d[:, t0:t0 + 512])], w=[("a_cs", sl)])

                a_load(0)
                for bi, (nm, S, b) in enumerate(ablocks):
                    if True:
                        if bi + 1 < len(ablocks):
                            a_load(bi + 1)
                        t0 = b * 512
                        n = 512
                        sl = bi % 2
                        xk = ("a_xb", sl)
                        norm_block(xb[sl], xk, hT, "a_hT", n, rtmp, rstd, "a")
                        if is_mla:
                            cosb, sinb = cosbs[sl], sinbs[sl]
                            ACS = ("a_cs", sl)
                            for m in range(5):
                                for c in range(8):
                                    op("pe", lambda e, m=m, c=c: e.matmul(ps[:, m, :], wd[:, c, m * 128:(m + 1) * 128], hT[:, c, :], start=(c == 0), stop=(c == 7)),
                                       r=["a_hT", "a_wd"], w=[PS(m)])
                            for m in range(2):
                                for c in range(8):
                                    op("pe", lambda e, m=m, c=c: e.matmul(ps[0:32, 5 + m, :], wd[:, c, 640 + 32 * m:672 + 32 * m], hT[:, c, :], start=(c == 0), stop=(c == 7)),
                                       r=["a_hT", "a_wd"], w=[PS(5 + m)])
                            for m in range(5):
                                op("act", lambda e, m=m: e.activation(out=lsq[:, m, :], in_=ps[:, m, :], func=AF.Square), r=[PS(m)], w=["a_lsq"])
                            for c in range(3):
                                op("pe", lambda e, c=c: e.matmul(ps[:, 7, :], ones_bf[:], lsq[:, c, :], start=(c == 0), stop=(c == 2)), r=["a_lsq", "ones_bf"], w=[PS(7)])
                            rstd_from_ps(7, 512, 384.0, lrs[:, 0, :], "a_lrs0", tmp=rtmp, tmpkey="rtmpa")
                            for c in range(2):
                                op("pe", lambda e, c=c: e.matmul(ps[:, 7, :], ones_bf[:], lsq[:, 3 + c, :], start=(c == 0), stop=(c == 1)), r=["a_lsq", "ones_bf"], w=[PS(7)])
                            rstd_from_ps(7, 512, 256.0, lrs[:, 1, :], "a_lrs1", tmp=rtmp, tmpkey="rtmpa")
                            for c in range(3):
                                op("dve", lambda e, c=c: e.tensor_tensor(out=cqn[:, c, :], in0=ps[:, c, :], in1=lrs[:, 0, :], op=ALU.mult), r=[PS(c), "a_lrs0"], w=["a_cqn"])
                            for c in range(2):
                                op("dve", lambda e, c=c: e.tensor_tensor(out=ckvn[:, c, :], in0=ps[:, 3 + c, :], in1=lrs[:, 1, :], op=ALU.mult), r=[PS(3 + c), "a_lrs1"], w=["a_ckvn"])
                            op("dve", lambda e: e.scalar_tensor_tensor(out=tA[:], in0=ps[0:32, 5, :], scalar=gcol(("g1k", j), 0, slice(0, 32)), in1=cosb[:], op0=ALU.mult, op1=ALU.mult),
                               r=[PS(5), "gv", ACS], w=["a_tA"])
                            op("dve", lambda e: e.scalar_tensor_tensor(out=tB[:], in0=ps[0:32, 6, :], scalar=gcol(("g2k", j), 0, slice(0, 32)), in1=sinb[:], op0=ALU.mult, op1=ALU.mult),
                               r=[PS(6), "gv", ACS], w=["a_tB"])
                            op("pool", lambda e: e.tensor_tensor(out=tK[:], in0=tA[:], in1=tB[:], op=ALU.add), r=["a_tA", "a_tB"], w=["a_tK"])
                            for side in ("k", "q"):
                                for h in range(8):
                                    pb = h % 2
                                    p1 = pb
                                    if side == "q":
                                        for c in range(3):
                                            op("pe", lambda e, c=c, h=h, p1=p1: e.matmul(ps[:, p1, :], wq1[:, c, h * 128:(h + 1) * 128], cqn[:, c, :], start=(c == 0), stop=(c == 2)),
                                               r=["a_cqn", "a_wq1"], w=[PS(p1)])
                                        p2 = 2 + pb
                                        for c in range(3):
                                            op("pe", lambda e, c=c, h=h, p2=p2: e.matmul(ps[0:32, p2, :], wq2[:, c, h * 32:(h + 1) * 32], cqn[:, c, :], start=(c == 0), stop=(c == 2)),
                                               r=["a_cqn", "a_wq2"], w=[PS(p2)])
                                        op("act", lambda e, p1=p1, pb=pb: e.activation(out=sqh[pb][:], in_=ps[:, p1, :], func=AF.Square), r=[PS(p1)], w=[("sqh", pb)])
                                    else:
                                        for c in range(2):
                                            op("pe", lambda e, c=c, h=h, p1=p1: e.matmul(ps[:, p1, :], wkn[:, c, h * 128:(h + 1) * 128], ckvn[:, c, :], start=(c == 0), stop=(c == 1)),
                                               r=["a_ckvn", "a_wkn"], w=[PS(p1)])
                                        op("act", lambda e, p1=p1, pb=pb: e.activation(out=sqh[pb][64:128, :], in_=ps[64:128, p1, :], func=AF.Square), r=[PS(p1)], w=[("sqh", pb)])
                                        op("act", lambda e, pb=pb: e.activation(out=sqh[pb][0:32, :], in_=ps[0:32, 5, :], func=AF.Square), r=[PS(5), ("sqh", pb)], w=[("sqh", pb)])
                                    sbk = 4 if pb == 0 else 7
                                    op("pe", lambda e, pb=pb, sbk=sbk: e.matmul(ps[:, sbk, :], ones_bf[:], sqh[pb][:], start=True, stop=True), r=[("sqh", pb), "ones_bf"], w=[PS(sbk)])
                                    rstd_from_ps(sbk, 512, 96.0, hr[pb][:], ("hr", pb), tmp=rtmp, tmpkey="rtmpa")
                                    qs = qoc % 4
                                    qoc += 1
                                    g1 = ("g1" + side, j)
                                    if side == "q":
                                        op("dve", lambda e, p1=p1, g1=g1: e.scalar_tensor_tensor(out=tA[:], in0=ps[0:32, p1, :], scalar=gcol(g1, 0, slice(0, 32)), in1=cosb[:], op0=ALU.mult, op1=ALU.mult),
                                           r=[PS(p1), "gv", ACS], w=["a_tA"])
                                        op("dve", lambda e, p2=p2: e.scalar_tensor_tensor(out=tB[:], in0=ps[0:32, p2, :], scalar=gcol(("g2q", j), 0, slice(0, 32)), in1=sinb[:], op0=ALU.mult, op1=ALU.mult),
                                           r=[PS(p2), "gv", ACS], w=["a_tB"])
                                        op("pool", lambda e: e.tensor_tensor(out=tA[:], in0=tA[:], in1=tB[:], op=ALU.add), r=["a_tA", "a_tB"], w=["a_tA"])
                                        op("pool", lambda e, qs=qs, pb=pb: e.tensor_tensor(out=qo[qs][0:32, :], in0=tA[:], in1=hr[pb][0:32, :], op=ALU.mult), r=["a_tA", ("hr", pb)], w=[("qo", qs)])
                                    else:
                                        op("pool", lambda e, qs=qs, pb=pb: e.tensor_tensor(out=qo[qs][0:32, :], in0=tK[:], in1=hr[pb][0:32, :], op=ALU.mult), r=["a_tK", ("hr", pb)], w=[("qo", qs)])
                                    op("dve", lambda e, p1=p1, qs=qs, pb=pb, g1=g1: e.scalar_tensor_tensor(out=qo[qs][64:128, :], in0=ps[64:128, p1, :], scalar=gcol(g1, 0, slice(64, 128)), in1=hr[pb][64:128, :], op0=ALU.mult, op1=ALU.mult),
                                       r=[PS(p1), "gv", ("hr", pb), ("qo", qs)], w=[("qo", qs)])
                                    dst = (QT if side == "q" else KT)[nm]
                                    dma("aq%d" % qs, lambda q, qs=qs, dst=dst, h=h, t0=t0: q.dma_start(out=dst[h, :, t0:t0 + 512], in_=qo[qs][:]),
                                        r=[("qo", qs)], w=[("QK", side, nm, h, b)])
                            for tt in range(4):
                                vs = vc % 2
                                vc += 1
                                for c in range(2):
                                    op("pe", lambda e, c=c, tt=tt: e.matmul(ps[:, 4, :], ckvn[:, c, tt * 128:(tt + 1) * 128], wv[:, c, :], start=(c == 0), stop=(c == 1)),
                                       r=["a_ckvn", "a_wv"], w=[PS(4)])
                                op("act", lambda e, vs=vs: e.activation(out=vsb[vs][:, :, 0:64], in_=ps[:, 4, :].rearrange("p (h d) -> p h d", h=8), func=AF.Copy), r=[PS(4)], w=[("vsb", vs)])
                                dma("av%d" % vs, lambda q, vs=vs, nm=nm, T=b * 4 + tt: q.dma_start(out=VD[nm][:, :, T, 0:65].rearrange("h p d -> p h d"), in_=vsb[vs][:]),
                                    r=[("vsb", vs)], w=[("VD", nm, b)])
                        else:
                            for side, cb in (("q", 0), ("k", 1024)):
                                for h in range(8):
                                    pb = h % 4
                                    p1 = pb
                                    for c in range(8):
                                        op("pe", lambda e, c=c, h=h, p1=p1, cb=cb: e.matmul(ps[:, p1, :], wqkv[:, c, cb + h * 128:cb + (h + 1) * 128], hT[:, c, :], start=(c == 0), stop=(c == 7)),
                                           r=["a_hT", "a_wqkv"], w=[PS(p1)])
                                    op("act", lambda e, p1=p1, pb=pb: e.activation(out=sqh[pb][:], in_=ps[:, p1, :], func=AF.Square), r=[PS(p1)], w=[("sqh", pb)])
                                    sbk = 4 + pb
                                    op("pe", lambda e, pb=pb, sbk=sbk: e.matmul(ps[:, sbk, :], ones2[:], sqh[pb][:], start=True, stop=True), r=[("sqh", pb), "ones2"], w=[PS(sbk)])
                                    rstd_from_ps(sbk, 512, 64.0, hr[pb][:], ("hr", pb), tmp=rtmp, tmpkey="rtmpa")
                                    qs = qoc % 4
                                    qoc += 1
                                    gk = ("d" + side, j)
                                    op("dve", lambda e, p1=p1, qs=qs, pb=pb, gk=gk: e.scalar_tensor_tensor(out=qo[qs][:], in0=ps[:, p1, :], scalar=gcol(gk), in1=hr[pb][:], op0=ALU.mult, op1=ALU.mult),
                                       r=[PS(p1), "gv", ("hr", pb)], w=[("qo", qs)])
                                    dst = (QT if side == "q" else KT)[nm]
                                    dma("aq%d" % qs, lambda q, qs=qs, dst=dst, h=h, t0=t0: q.dma_start(out=dst[h, :, t0:t0 + 512], in_=qo[qs][:]),
                                        r=[("qo", qs)], w=[("QK", side, nm, h, b)])
                            for tt in range(4):
                                vs = vc % 2
                                vc += 1
                                for half in range(2):
                                    bk = 2 + half
                                    for c in range(8):
                                        op("pe", lambda e, c=c, tt=tt, half=half, bk=bk: e.matmul(ps[:, bk, :], hT[:, c, tt * 128:(tt + 1) * 128], wqkv[:, c, 2048 + half * 512:2048 + (half + 1) * 512], start=(c == 0), stop=(c == 7)),
                                           r=["a_hT", "a_wqkv"], w=[PS(bk)])
                                op("act", lambda e, vs=vs: e.activation(out=vsb[vs][:], in_=ps[:, 2:4, :].rearrange("p a b -> p (a b)"), func=AF.Copy), r=[PS(2), PS(3)], w=[("vsb", vs)])
                                dma("av%d" % vs, lambda q, vs=vs, nm=nm, T=b * 4 + tt: q.dma_start(out=VD[nm][:, :, T, 0:128].rearrange("h p d -> p h d"), in_=vsb[vs][:].rearrange("p (h d) -> p h d", h=8)),
                                    r=[("vsb", vs)], w=[("VD", nm, b)])
                mk.barrier()

            with ExitStack() as st:
                heads = [(nm, S, h) for nm, S in seqs for h in range(8)]
                steps = []

                def run_pipeline(LA):
                    nst = len(steps)
                    for i in range(nst + LA):
                        if i < nst:
                            for f in steps[i]["pre"]:
                                f()
                            steps[i]["qk"]()
                            steps[i]["ex"]()
                        if i >= LA:
                            steps[i - LA]["pv"]()
                            for f in steps[i - LA]["post"]:
                                f()

                if is_mla:
                    kt = [sb("b_kt%d" % i, [128, smax], BF16, st) for i in range(2)]
                    vt = [sb("b_vt%d" % i, [128, smax // 128, 65], BF16, st) for i in range(2)]
                    qb = [sb("b_q%d" % i, [128, 512], BF16, st) for i in range(2)]
                    pt = [sb("b_p%d" % i, [128, 1024], BF16, st) for i in range(3)]
                    osb = sb("b_osb", [128, 512], F32, st)
                    rr = sb("b_rr", [128, 512], F32, st)
                    oo = [sb("b_oo%d" % i, [64, 512], BF16, st) for i in range(2)]

                    def load_head(hi):
                        nm, S, h = heads[hi]
                        hs = hi % 2
                        nkt = S // 128
                        dma("bk%d" % hs, lambda q: q.dma_start(out=kt[hs][:, 0:S], in_=KT[nm][h, :, :]),
                            r=[("QK", "k", nm, h, bb) for bb in range(S // 512)], w=[("b_kt", hs)])
                        dma("bv%d" % hs, lambda q: q.dma_start(out=vt[hs][:, 0:nkt, :], in_=VD[nm][h, :, :, 0:65]),
                            r=[("VD", nm, bb) for bb in range(S // 512)], w=[("b_vt", hs)])

                    qblocks = [(hi, b) for hi, (nm, S, h) in enumerate(heads) for b in range(S // 512)]

                    def load_q(qi):
                        hi, b = qblocks[qi]
                        nm, S, h = heads[hi]
                        qs = qi % 2
                        dma("bq%d" % qs, lambda q: q.dma_start(out=qb[qs][:], in_=QT[nm][h, :, b * 512:(b + 1) * 512]),
                            r=[("QK", "q", nm, h, b)], w=[("b_q", qs)])

                    gstep = 0
                    for qi, (hi, b) in enumerate(qblocks):
                        nm, S, h = heads[hi]
                        hs = hi % 2
                        qs = qi % 2
                        nkt = S // 128
                        npair = nkt // 2
                        ob = 6 + (qi % 2)
                        os_ = qi % 2
                        t0 = b * 512
                        for kp in range(npair):
                            pre = []
                            if kp == 0:
                                if qi == 0:
                                    pre.append(lambda: load_head(0))
                                    if len(heads) > 1:
                                        pre.append(lambda: load_head(1))
                                    pre.append(lambda: load_q(0))
                                if qi + 1 < len(qblocks):
                                    pre.append(lambda qi=qi: load_q(qi + 1))
                            sl3 = gstep % 3
                            gstep += 1
                            sbk = 2 * sl3

                            def qk(hs=hs, qs=qs, kp=kp, sbk=sbk):
                                for u in range(2):
                                    k0 = (2 * kp + u) * 128
                                    op("pe", lambda e, k0=k0, bk=sbk + u: e.matmul(ps[:, bk, :], kt[hs][:, k0:k0 + 128], qb[qs][:], start=True, stop=True),
                                       r=[("b_kt", hs), ("b_q", qs)], w=[PS(sbk + u)])

                            def ex(sbk=sbk, sl3=sl3):
                                op("act", lambda e: e.activation(out=pt[sl3][:], in_=ps[:, sbk:sbk + 2, :].rearrange("p a b -> p (a b)"), func=AF.Exp, scale=96.0 ** -0.5),
                                   r=[PS(sbk), PS(sbk + 1)], w=[("b_p", sl3)])

                            def pv(hs=hs, kp=kp, sl3=sl3, ob=ob, nkt=nkt):
                                for u in range(2):
                                    kk = 2 * kp + u
                                    op("pe", lambda e, kk=kk, u=u: e.matmul(ps[0:65, ob, :], vt[hs][:, kk, :], pt[sl3][:, u * 512:(u + 1) * 512], start=(kk == 0), stop=(kk == nkt - 1)),
                                       r=[("b_vt", hs), ("b_p", sl3)], w=[PS(ob)])

                            post = []
                            if kp == npair - 1:
                                def fin(ob=ob, os_=os_, nm=nm, h=h, t0=t0, b=b):
                                    op("dve", lambda e: e.tensor_copy(out=osb[0:65, :], in_=ps[0:65, ob, :]), r=[PS(ob)], w=["b_osb"])
                                    op("dve", lambda e: e.reciprocal(out=rr[64:65, :], in_=osb[64:65, :]), r=["b_osb"], w=["b_rr"])
                                    op("pe", lambda e: e.matmul(ps[0:64, ob, :], ones_f[64:65, 0:64], rr[64:65, :], start=True, stop=True), r=["b_rr", "ones_f", "b_osb"], w=[PS(ob)])
                                    op("dve", lambda e: e.tensor_tensor(out=oo[os_][:], in0=osb[0:64, :], in1=ps[0:64, ob, :], op=ALU.mult), r=["b_osb", PS(ob)], w=[("b_oo", os_)])
                                    dma("bo%d" % os_, lambda q: q.dma_start(out=OT[nm][h // 2, (h % 2) * 64:(h % 2) * 64 + 64, t0:t0 + 512], in_=oo[os_][:]),
                                        r=[("b_oo", os_)], w=[("OT", nm, b, h)])
                                post.append(fin)
                                if b == S // 512 - 1 and hi + 2 < len(heads):
                                    post.append(lambda hi=hi: load_head(hi + 2))
                            steps.append(dict(pre=pre, qk=qk, ex=ex, pv=pv, post=post))
                    run_pipeline(2)
                else:
                    kt = [sb("b_kt%d" % i, [68, 2, smax], BF16, st) for i in range(2)]
                    vt = [sb("b_vt%d" % i, [128, smax // 128, 128], BF16, st) for i in range(2)]
                    qL = [sb("b_qL%d" % i, [68, 2, 512], BF16, st) for i in range(2)]
                    qR = [sb("b_qR%d" % i, [68, 2, 512], BF16, st) for i in range(2)]
                    pt = [sb("b_p%d" % i, [128, 1024], BF16, st) for i in range(3)]
                    adg = sb("b_adg", [128, 4, 512], F32, st)
                    r0 = sb("b_r0", [128, 2, 512], F32, st)
                    sacc = [sb("b_sacc%d" % i, [128, 2, 512], F32, st) for i in range(2)]
                    o0 = sb("b_o0", [128, 2, 512], F32, st)
                    osq = sb("b_osq", [128, 512], BF16, st)
                    ors = sb("b_ors", [128, 512], F32, st)
                    ortmp = sb("b_ortmp", [128, 512], F32, st)
                    oo = [sb("b_oo%d" % i, [128, 512], BF16, st) for i in range(2)]
                    lamt = sb("b_lamt", [1, 256], F32, st)
                    lamp = sb("b_lamp", [1, 128], F32, st)
                    lams = sb("b_lams", [1, 4], F32, st)
                    nlam = sb("b_nlam", [128, 1], F32, st)
                    gsub = sb("b_gsub", [128, 1], F32, st)
                    lambda_init = 0.8 - 0.6 * math.exp(-0.3 * L)
                    dma("c0", lambda q: q.dma_start(out=adg[:], in_=ald_d[:, :, :]), w=["b_adg"])
                    for i in range(2):
                        dma("c1", [lambda q, i=i, mm=mm: q.dma_start(out=qL[i][64:68, mm, :], in_=alq_d[0:4, :]) for mm in range(2)] +
                                  [lambda q, i=i, mm=mm: q.dma_start(out=qR[i][64:68, mm, :], in_=alq_d[4:8, :]) for mm in range(2)], w=[("b_qa", i)])
                    dma("c0", lambda q: q.dma_start(out=lamt[:], in_=W["diff_lam"][j]), w=["b_lamt"])
                    op("dve", lambda e: e.tensor_tensor(out=lamp[:].rearrange("p (a d) -> p a d", a=2), in0=lamt[:].rearrange("p (a t d) -> p a t d", a=2, t=2)[:, :, 0, :],
                                                        in1=lamt[:].rearrange("p (a t d) -> p a t d", a=2, t=2)[:, :, 1, :], op=ALU.mult), r=["b_lamt"], w=["b_lamp"])
                    op("dve", lambda e: e.tensor_reduce(out=lams[:, 0:2], in_=lamp[:].rearrange("p (a d) -> p a d", a=2), axis=mybir.AxisListType.X, op=ALU.add), r=["b_lamp"], w=["b_lams"])
                    op("act", lambda e: e.activation(out=lams[:, 2:4], in_=lams[:, 0:2], func=AF.Exp), r=["b_lams"], w=["b_lams2"])
                    op("dve", lambda e: e.scalar_tensor_tensor(out=lams[:, 0:1], in0=lams[:, 3:4], scalar=-lambda_init, in1=lams[:, 2:3], op0=ALU.add, op1=ALU.subtract), r=["b_lams2", "b_lams"], w=["b_lams"])
                    op("pe", lambda e: e.matmul(ps[:, 0, 0:1], ones_f[0:1, :], lams[0:1, 0:1], start=True, stop=True), r=["b_lams", "ones_f"], w=[PS(0)])
                    op("dve", lambda e: e.tensor_copy(out=nlam[:], in_=ps[:, 0, 0:1]), r=[PS(0)], w=["b_nlam"])
                    op("dve", lambda e: e.tensor_scalar(out=gsub[:], in0=gcol(("dsub", j)), scalar1=float(1.0 - lambda_init), scalar2=None, op0=ALU.mult), r=["gv"], w=["b_gsub"])

                    def load_head(hi):
                        nm, S, h = heads[hi]
                        hs = hi % 2
                        nkt = S // 128
                        dma("bk%d" % hs, [lambda q, mm=mm: q.dma_start(out=kt[hs][0:64, mm, 0:S], in_=KT[nm][h, mm * 64:(mm + 1) * 64, :]) for mm in range(2)] +
                                         [lambda q, mm=mm: q.dma_start(out=kt[hs][64:68, mm, 0:S], in_=alk_d[h, :, 0:S]) for mm in range(2)],
                            r=[("QK", "k", nm, h, bb) for bb in range(S // 512)], w=[("b_kt", hs)])
                        dma("bv%d" % hs, lambda q: q.dma_start(out=vt[hs][:, 0:nkt, :], in_=VD[nm][h, :, :, 0:128]),
                            r=[("VD", nm, bb) for bb in range(S // 512)], w=[("b_vt", hs)])

                    qblocks = [(hi, b) for hi, (nm, S, h) in enumerate(heads) for b in range(S // 512)]

                    def load_q(qi):
                        hi, b = qblocks[qi]
                        nm, S, h = heads[hi]
                        qs = qi % 2
                        dma("bq%d" % qs, [lambda q, mm=mm, dst=dst: q.dma_start(out=dst[qs][0:64, mm, :], in_=QT[nm][h, mm * 64:(mm + 1) * 64, b * 512:(b + 1) * 512])
                                          for mm in range(2) for dst in (qL, qR)],
                            r=[("QK", "q", nm, h, b)], w=[("b_q", qs)])

                    BAND_T = 50.0
                    gstep = 0
                    for qi, (hi, b) in enumerate(qblocks):
                        nm, S, h = heads[hi]
                        hs = hi % 2
                        qs = qi % 2
                        nkt = S // 128
                        os_ = qi % 2
                        t0 = b * 512
                        slope = 2.0 ** (-(h + 1))
                        incl = []
                        for kk in range(nkt):
                            kb = kk // 4
                            if kb < b:
                                dmin = t0 - (kk * 128 + 127)
                            elif kb > b:
                                dmin = kk * 128 - (t0 + 511)
                            else:
                                dmin = 0
                            if slope * dmin <= BAND_T:
                                incl.append(kk)
                        for ii, kk in enumerate(incl):
                            first = (ii == 0)
                            last = (ii == len(incl) - 1)
                            pre = []
                            if first:
                                if qi == 0:
                                    pre.append(lambda: load_head(0))
                                    if len(heads) > 1:
                                        pre.append(lambda: load_head(1))
                                    pre.append(lambda: load_q(0))
                                if qi + 1 < len(qblocks):
                                    pre.append(lambda qi=qi: load_q(qi + 1))
                            sl2 = gstep % 2
                            sl3 = gstep % 3
                            gstep += 1
                            ssl = 2 * sl2
                            kb = kk // 4
                            k0 = kk * 128

                            def qk(hs=hs, qs=qs, kk=kk, kb=kb, k0=k0, ssl=ssl, b=b, slope=slope):
                                for mm in range(2):
                                    bk = ssl + mm
                                    if kb == b:
                                        op("pe", lambda e, bk=bk, mm=mm: e.matmul(ps[:, bk, :], kt[hs][0:64, mm, k0:k0 + 128], qL[qs][0:64, mm, :], start=True, stop=True),
                                           r=[("b_kt", hs), ("b_q", qs)], w=[PS(bk)])
                                        op("dve", lambda e, bk=bk, jj=kk % 4: e.scalar_tensor_tensor(out=ps[:, bk, :], in0=adg[:, jj, :], scalar=float(-slope * 8.0), in1=ps[:, bk, :], op0=ALU.mult, op1=ALU.add),
                                           r=["b_adg", PS(bk)], w=[PS(bk)])
                                    else:
                                        qsrc = qL if kb < b else qR
                                        op("pe", lambda e, bk=bk, mm=mm, qsrc=qsrc: e.matmul(ps[:, bk, :], kt[hs][:, mm, k0:k0 + 128], qsrc[qs][:, mm, :], start=True, stop=True),
                                           r=[("b_kt", hs), ("b_q", qs), ("b_qa", qs)], w=[PS(bk)])

                            def ex(ssl=ssl, sl3=sl3, kb=kb, b=b, slope=slope):
                                cb = float(-slope * 512.0 * abs(b - kb))
                                for mm in range(2):
                                    op("act", lambda e, mm=mm: e.activation(out=pt[sl3][:, mm * 512:(mm + 1) * 512], in_=ps[:, ssl + mm, :], func=AF.Exp, scale=0.125, bias=cb),
                                       r=[PS(ssl + mm)], w=[("b_p", sl3, mm)])

                            def pv(hs=hs, kk=kk, sl3=sl3, first=first, last=last, asl=qi % 2):
                                for mm in range(2):
                                    op("pe", lambda e, mm=mm: e.matmul(ps[:, 4 + mm, :], vt[hs][:, kk, :], pt[sl3][:, mm * 512:(mm + 1) * 512], start=first, stop=last),
                                       r=[("b_vt", hs), ("b_p", sl3, mm)], w=[PS(4 + mm)])
                                    if mm == 0:
                                        op("pe", lambda e, mm=mm: e.matmul(ps[:, 6, :], ones_bf[:], pt[sl3][:, 0:512], start=first, stop=last),
                                           r=["ones_bf", ("b_p", sl3, mm)], w=[PS(6)])
                                    elif first:
                                        op("dve", lambda e, mm=mm: e.tensor_copy(out=sacc[asl][:, mm, :], in_=pt[sl3][:, mm * 512:(mm + 1) * 512]), r=[("b_p", sl3, mm)], w=[("b_sacc", asl, mm)])
                                    else:
                                        op("dve", lambda e, mm=mm: e.tensor_tensor(out=sacc[asl][:, mm, :], in0=sacc[asl][:, mm, :], in1=pt[sl3][:, mm * 512:(mm + 1) * 512], op=ALU.add),
                                           r=[("b_p", sl3, mm), ("b_sacc", asl, mm)], w=[("b_sacc", asl, mm)])

                            post = []
                            if last:
                                def fin(os_=os_, nm=nm, h=h, t0=t0, b=b, asl=qi % 2):
                                    op("dve", lambda e: e.tensor_copy(out=o0[:].rearrange("p a b -> p (a b)"), in_=ps[:, 4:6, :].rearrange("p a b -> p (a b)")), r=[PS(4), PS(5)], w=["b_o0"])
                                    op("pe", lambda e: e.matmul(ps[:, 7, :], ones_f[:], sacc[asl][:, 1, :], start=True, stop=True), r=["ones_f", ("b_sacc", asl, 1)], w=[PS(7)])
                                    op("act", lambda e: e.activation(out=r0[:].rearrange("p a b -> p (a b)"), in_=ps[:, 6:8, :].rearrange("p a b -> p (a b)"), func=AF.Ln), r=[PS(6), PS(7)], w=["b_r0"])
                                    op("act", lambda e: e.activation(out=r0[:].rearrange("p a b -> p (a b)"), in_=r0[:].rearrange("p a b -> p (a b)"), func=AF.Exp, scale=-1.0), r=["b_r0"], w=["b_r0"])
                                    op("dve", lambda e: e.tensor_tensor(out=o0[:].rearrange("p a b -> p (a b)"), in0=o0[:].rearrange("p a b -> p (a b)"), in1=r0[:].rearrange("p a b -> p (a b)"), op=ALU.mult),
                                       r=["b_o0", "b_r0"], w=["b_o0"])
                                    op("dve", lambda e: e.scalar_tensor_tensor(out=o0[:, 0, :], in0=o0[:, 1, :], scalar=nlam[:, 0:1], in1=o0[:, 0, :], op0=ALU.mult, op1=ALU.add), r=["b_o0", "b_nlam"], w=["b_o0"])
                                    op("pool", lambda e: e.tensor_tensor(out=osq[:], in0=o0[:, 0, :], in1=o0[:, 0, :], op=ALU.mult), r=["b_o0"], w=["b_osq"])
                                    op("pe", lambda e: e.matmul(ps[:, 7, :], ones_bf[:], osq[:], start=True, stop=True), r=["b_osq", "ones_bf", "b_r0"], w=[PS(7)])
                                    rstd_from_ps(7, 512, 128.0, ors[:], "b_ors", tmp=ortmp, tmpkey="b_ortmp")
                                    op("dve", lambda e: e.scalar_tensor_tensor(out=oo[os_][:], in0=o0[:, 0, :], scalar=gsub[:, 0:1], in1=ors[:], op0=ALU.mult, op1=ALU.mult),
                                       r=["b_o0", "b_gsub", "b_ors", PS(7)], w=[("b_oo", os_)])
                                    dma("bo%d" % os_, lambda q: q.dma_start(out=OT[nm][h, :, t0:t0 + 512], in_=oo[os_][:]),
                                        r=[("b_oo", os_)], w=[("OT", nm, b, h)])
                                post.append(fin)
                                if b == S // 512 - 1 and hi + 2 < len(heads):
                                    post.append(lambda hi=hi: load_head(hi + 2))
                            steps.append(dict(pre=pre, qk=qk, ex=ex, pv=pv, post=post))
                    run_pipeline(1)
                mk.barrier()

            with ExitStack() as st:
                nko = 4 if is_mla else 8
                wo = sb("c_wo", [128, nko, 1024], BF16, st)
                load_w(wo, "c_wo", (W["mla_wo"] if is_mla else W["diff_wo"])[j], nko * 128, 1024)
                wxq = sb("c_wxq", [128, 8, 1024], BF16, st)
                wxo = sb("c_wxo", [128, 8, 1024], BF16, st)
                load_w(wxq, "c_wxq", W["xa_wq"][L], 1024, 1024, gain=("xa_norm", L))
                load_w(wxo, "c_wxo", W["xa_wo"][L], 1024, 1024)
                mkT = {nm: sb("c_mk_" + nm, [128, 8, NMEM], BF16, st) for nm, _ in seqs}
                mv = {nm: sb("c_mv_" + nm, [128, 2, 1024], BF16, st) for nm, _ in seqs}
                hr = sb("c_hr", [128, 512], F32, st)
                rtmp = sb("c_rtmp", [128, 512], F32, st)
                with ExitStack() as st2:
                    wkv = sb("c_wkv", [128, 8, 2048], BF16, st2)
                    load_w(wkv, "c_wkv", W["xa_wkv"][L], 1024, 2048, gain=("xa_mem", L))
                    ksq = sb("c_ksq", [128, 2, NMEM], BF16, st2)
                    memn = {nm: sb("c_memn_" + nm, [128, 8, NMEM], BF16, st2) for nm, _ in seqs}
                    for mi, (nm, _) in enumerate(seqs):
                        dma("cm%d" % (mi % 2), lambda q, nm=nm: q.dma_start(out=memn[nm][:], in_=MEMK[nm][:, :, :]), r=[("memn", nm)], w=[("c_memn", nm)])
                    for nm, _ in seqs:
                        for hd in range(4):
                            for dc in range(2):
                                m = hd * 2 + dc
                                for c in range(8):
                                    op("pe", lambda e, c=c, m=m, dc=dc, nm=nm: e.matmul(ps[:, dc, 0:NMEM], wkv[:, c, m * 128:(m + 1) * 128], memn[nm][:, c, :], start=(c == 0), stop=(c == 7)),
                                       r=["c_wkv", ("c_memn", nm)], w=[PS(dc)])
                                op("act", lambda e, dc=dc: e.activation(out=ksq[:, dc, :], in_=ps[:, dc, 0:NMEM], func=AF.Square), r=[PS(dc)], w=["c_ksq"])
                            for dc in range(2):
                                op("pe", lambda e, dc=dc: e.matmul(ps[:, 2, 0:NMEM], ones_bf[:], ksq[:, dc, :], start=(dc == 0), stop=(dc == 1)), r=["c_ksq", "ones_bf"], w=[PS(2)])
                            rstd_from_ps(2, NMEM, 256.0, hr[:, 0:NMEM], "c_hr", tmp=rtmp, tmpkey="rtmpc")
                            for dc in range(2):
                                op("dve", lambda e, dc=dc, hd=hd, nm=nm: e.scalar_tensor_tensor(out=mkT[nm][:, hd * 2 + dc, :], in0=ps[:, dc, 0:NMEM], scalar=gcol(("xk", L), dc), in1=hr[:, 0:NMEM], op0=ALU.mult, op1=ALU.mult),
                                   r=[PS(dc), "gv", "c_hr"], w=[("c_mk", nm)])
                        for ktile in range(2):
                            for half in range(2):
                                for c in range(8):
                                    op("pe", lambda e, c=c, ktile=ktile, half=half, nm=nm: e.matmul(ps[:, 4 + half, :], memn[nm][:, c, ktile * 128:(ktile + 1) * 128], wkv[:, c, 1024 + half * 512:1024 + (half + 1) * 512], start=(c == 0), stop=(c == 7)),
                                       r=["c_wkv", ("c_memn", nm)], w=[PS(4 + half)])
                            op("act", lambda e, ktile=ktile, nm=nm: e.activation(out=mv[nm][:, ktile, :], in_=ps[:, 4:6, :].rearrange("p a b -> p (a b)"), func=AF.Copy), r=[PS(4), PS(5)], w=[("c_mv", nm)])
                    mk.barrier()
                xb = [sb("c_xb%d" % i, [128, 8, 512], F32, st) for i in range(2)]
                ob = [sb("c_ob%d" % i, [128, nko, 512], BF16, st) for i in range(2)]
                hT = sb("c_hT", [128, 8, 512], BF16, st)
                rstd = sb("c_rstd", [128, 512], F32, st)
                qsq = sb("c_qsq", [128, 2, 512], BF16, st)
                qn2 = [sb("c_qn%d" % i, [128, 2, 512], BF16, st) for i in range(2)]
                pt = [sb("c_p%d" % i, [128, 2, 512], BF16, st) for i in range(2)]
                rs = sb("c_rs", [128, 512], F32, st)
                on = sb("c_on", [128, 8, 512], BF16, st)
                cblocks = [(nm, S, b) for nm, S in seqs for b in range(S // 512)]

                def c_load(bi):
                    nm, S, b = cblocks[bi]
                    sl = bi % 2
                    t0 = b * 512
                    dma("cx%d" % sl, lambda q: q.dma_start(out=xb[sl][:], in_=X0[nm][:, :, t0:t0 + 512]), r=xkeys("X0", nm, t0, t0 + 512), w=[("c_xb", sl)])
                    dma("co%d" % sl, lambda q: q.dma_start(out=ob[sl][:], in_=OT[nm][0:nko, :, t0:t0 + 512].rearrange("c p t -> p c t")),
                        r=[("OT", nm, b, h) for h in range(8)], w=[("c_ob", sl)])

                c_load(0)
                for bi, (nm, S, b) in enumerate(cblocks):
                    if True:
                        if bi + 1 < len(cblocks):
                            c_load(bi + 1)
                        t0 = b * 512
                        sl = bi % 2
                        xk = ("c_xb", sl)
                        for m in range(8):
                            bk = m % 2
                            for c in range(nko):
                                op("pe", lambda e, c=c, m=m, bk=bk, sl=sl: e.matmul(ps[:, bk, :], wo[:, c, m * 128:(m + 1) * 128], ob[sl][:, c, :], start=(c == 0), stop=(c == nko - 1)),
                                   r=["c_wo", ("c_ob", sl)], w=[PS(bk)])
                            op("dve", lambda e, m=m, bk=bk, sl=sl: e.tensor_tensor(out=xb[sl][:, m, :], in0=xb[sl][:, m, :], in1=ps[:, bk, :], op=ALU.add), r=[xk, PS(bk)], w=[xk])
                        norm_block(xb[sl], xk, hT, "c_hT", 512, rtmp, rstd, "c")
                        def xa_s1(hd):
                            qb_ = hd % 2
                            for dc in range(2):
                                m = hd * 2 + dc
                                for c in range(8):
                                    op("pe", lambda e, c=c, m=m, dc=dc: e.matmul(ps[:, dc, :], wxq[:, c, m * 128:(m + 1) * 128], hT[:, c, :], start=(c == 0), stop=(c == 7)),
                                       r=["c_wxq", "c_hT"], w=[PS(dc)])
                            op("act", lambda e: e.activation(out=qsq[:].rearrange("p a b -> p (a b)"), in_=ps[:, 0:2, :].rearrange("p a b -> p (a b)"), func=AF.Square), r=[PS(0), PS(1)], w=["c_qsq"])
                            for dc in range(2):
                                op("pe", lambda e, dc=dc: e.matmul(ps[:, 2, :], ones_bf[:], qsq[:, dc, :], start=(dc == 0), stop=(dc == 1)), r=["c_qsq", "ones_bf"], w=[PS(2)])
                            rstd_from_ps(2, 512, 256.0, hr[:], "c_hr", tmp=rtmp, tmpkey="rtmpc")
                            for dc in range(2):
                                op("dve", lambda e, dc=dc: e.scalar_tensor_tensor(out=qn2[qb_][:, dc, :], in0=ps[:, dc, :], scalar=gcol(("xq", L), dc), in1=hr[:], op0=ALU.mult, op1=ALU.mult),
                                   r=[PS(dc), "gv", "c_hr"], w=[("c_qn", qb_)])

                        def xa_s2(hd, nm=nm):
                            qb_ = hd % 2
                            psl = hd % 2
                            for ktile in range(2):
                                for dc in range(2):
                                    op("pe", lambda e, ktile=ktile, dc=dc: e.matmul(ps[:, 3 + ktile, :], mkT[nm][:, hd * 2 + dc, ktile * 128:(ktile + 1) * 128], qn2[qb_][:, dc, :], start=(dc == 0), stop=(dc == 1)),
                                       r=[("c_mk", nm), ("c_qn", qb_)], w=[PS(3 + ktile)])
                            op("act", lambda e: e.activation(out=pt[psl][:].rearrange("p a b -> p (a b)"), in_=ps[:, 3:5, :].rearrange("p a b -> p (a b)"), func=AF.Exp, scale=1.0 / 16.0),
                               r=[PS(3), PS(4)], w=[("c_p", psl)])
                            for ktile in range(2):
                                op("pe", lambda e, ktile=ktile: e.matmul(ps[:, 5, :], ones_bf[:], pt[psl][:, ktile, :], start=(ktile == 0), stop=(ktile == 1)), r=[("c_p", psl), "ones_bf"], w=[PS(5)])
                            for dc in range(2):
                                for ktile in range(2):
                                    op("pe", lambda e, ktile=ktile, dc=dc: e.matmul(ps[:, 6 + dc, :], mv[nm][:, ktile, hd * 256 + dc * 128:hd * 256 + (dc + 1) * 128], pt[psl][:, ktile, :], start=(ktile == 0), stop=(ktile == 1)),
                                       r=[("c_mv", nm), ("c_p", psl)], w=[PS(6 + dc)])
                            op("act", lambda e: e.activation(out=rs[:], in_=ps[:, 5, :], func=AF.Ln), r=[PS(5)], w=["c_rs"])
                            op("act", lambda e: e.activation(out=rs[:], in_=rs[:], func=AF.Exp, scale=-1.0), r=["c_rs"], w=["c_rs"])
                            for dc in range(2):
                                op("dve", lambda e, dc=dc: e.tensor_tensor(out=on[:, hd * 2 + dc, :], in0=ps[:, 6 + dc, :], in1=rs[:], op=ALU.mult), r=[PS(6 + dc), "c_rs"], w=["c_on"])

                        xa_s1(0)
                        for hd in range(4):
                            if hd + 1 < 4:
                                xa_s1(hd + 1)
                            xa_s2(hd)
                        for m in range(8):
                            bk = m % 2
                            for c in range(8):
                                op("pe", lambda e, c=c, m=m, bk=bk: e.matmul(ps[:, bk, :], wxo[:, c, m * 128:(m + 1) * 128], on[:, c, :], start=(c == 0), stop=(c == 7)),
                                   r=["c_wxo", "c_on"], w=[PS(bk)])
                            op("dve", lambda e, m=m, bk=bk, sl=sl: e.tensor_tensor(out=xb[sl][:, m, :], in0=xb[sl][:, m, :], in1=ps[:, bk, :], op=ALU.add), r=[xk, PS(bk)], w=[xk])
                        dma("cs%d" % sl, lambda q, sl=sl, nm=nm, t0=t0: q.dma_start(out=X1[nm][:, :, t0:t0 + 512], in_=xb[sl][:]), r=[xk], w=xkeys("X1", nm, t0, t0 + 512))
                mk.barrier()

            with ExitStack() as st:
                wgu = sb("f_wgu", [128, 8, 2 * DFF], BF16, st)
                load_w(wgu, "f_wgu", W["ffn_wgu"][L], 1024, 2 * DFF, gain=("ffn_norm", L))
                NH = NFC // 2
                wdn = [sb("f_wdn%d" % i, [128, NH, 128], BF16, st) for i in range(2)]
                xb = [sb("f_xb%d" % i, [128, 8, 512], F32, st) for i in range(2)]
                hT = sb("f_hT", [128, 8, 512], BF16, st)
                rstd = sb("f_rstd", [128, 512], F32, st)
                rtmp = sb("f_rtmp", [128, 512], F32, st)
                acc = [sb("f_acc%d" % i, [128, 512], F32, st) for i in range(3)]
                sg = [sb("f_sg%d" % i, [128, 512], F32, st) for i in range(3)]
                mT = sb("f_mT", [128, NH, 512], BF16, st)
                wc = 0
                for half in range(2):
                    for m in range(8):
                        wsl = wcnt[0] % 2
                        wcnt[0] += 1
                        ws = wc % 2
                        wc += 1
                        dma("w%d" % wsl, lambda q, wsl=wsl, m=m, half=half: q.dma_start(out=wst[wsl][:, 0:NH * 128].rearrange("p (c n) -> p c n", c=NH),
                                                                                      in_=W["ffn_wdn"][L][half * NH * 128:(half + 1) * NH * 128, m * 128:(m + 1) * 128].rearrange("(c p) n -> p c n", p=128)),
                            w=[("wst", wsl)])
                        op("dve" if (wc % 2) else "pool", lambda e, ws=ws, wsl=wsl: e.tensor_copy(out=wdn[ws][:].rearrange("p c n -> p (c n)"), in_=wst[wsl][:, 0:NH * 128]), r=[("wst", wsl)], w=[("f_wdn", ws)])
                        dma("fwb%d" % ws, lambda q, ws=ws, m=m, half=half: q.dma_start(out=WDB[half, m], in_=wdn[ws][:]), r=[("f_wdn", ws)], w=[("WDB", half, m)])
                fblocks = []
                for nm, S in seqs:
                    nblk = -(-S // 510)
                    wout = -(-S // nblk)
                    wout = -(-wout // 2) * 2
                    for b in range(nblk):
                        a0 = b * wout
                        a1 = min(S, a0 + wout)
                        l0 = max(0, a0 - 1)
                        l1 = min(S, a1 + 1)
                        padl = 1 if a0 == 0 else 0
                        padr = 1 if a1 == S else 0
                        n = (l1 - l0) + padl + padr
                        no = a1 - a0
                        assert n == no + 2 and n <= 512
                        fblocks.append((nm, S, a0, a1, l0, l1, padl, padr, n, no))

                def f_load(bi):
                    nm, S, a0, a1, l0, l1, padl, padr, n, no = fblocks[bi]
                    sl = bi % 2
                    xk = ("f_xb", sl)
                    if padl:
                        op("pool", lambda e: e.memset(xb[sl][:, :, 0:1], 0.0), w=[xk])
                    if padr:
                        op("pool", lambda e: e.memset(xb[sl][:, :, n - 1:n], 0.0), w=[xk])
                    dma("fx%d" % sl, lambda q: q.dma_start(out=xb[sl][:, :, padl:padl + l1 - l0], in_=X1[nm][:, :, l0:l1]),
                        r=xkeys("X1", nm, l0, l1), w=[xk])

                f_load(0)
                for bi, (nm, S, a0, a1, l0, l1, padl, padr, n, no) in enumerate(fblocks):
                    if True:
                        if bi + 1 < len(fblocks):
                            f_load(bi + 1)
                        sl = bi % 2
                        xk = ("f_xb", sl)
                        norm_block(xb[sl], xk, hT, "f_hT", n, rtmp, rstd, "f")
                        for half in range(2):
                            for fi in range(NH):
                                fc = half * NH + fi
                                ab = fc % 3
                                gb, ub = 2 * ab, 2 * ab + 1
                                for c in range(8):
                                    op("pe", lambda e, c=c, fc=fc, gb=gb, n=n: e.matmul(ps[:, gb, 0:n], wgu[:, c, fc * 128:(fc + 1) * 128], hT[:, c, 0:n], start=(c == 0), stop=(c == 7)),
                                       r=["f_wgu", "f_hT"], w=[PS(gb)])
                                for c in range(8):
                                    op("pe", lambda e, c=c, fc=fc, ub=ub, no=no: e.matmul(ps[:, ub, 0:no], wgu[:, c, DFF + fc * 128:DFF + (fc + 1) * 128], hT[:, c, 1:1 + no], start=(c == 0), stop=(c == 7)),
                                       r=["f_wgu", "f_hT"], w=[PS(ub)])
                                cw0 = gcol(("cw", L), 0 * NFC + fc); cw1 = gcol(("cw", L), 1 * NFC + fc); cw2 = gcol(("cw", L), 2 * NFC + fc)
                                cbv = gcol(("cb", L), fc)
                                op("act", lambda e, gb=gb, ab=ab, no=no, cw1=cw1, cbv=cbv: e.activation(out=acc[ab][:, 0:no], in_=ps[:, gb, 1:1 + no], func=AF.Identity, scale=cw1, bias=cbv),
                                   r=[PS(gb), "gv"], w=[("f_acc", ab)])
                                op("dve", lambda e, gb=gb, ab=ab, no=no, cw0=cw0: e.scalar_tensor_tensor(out=acc[ab][:, 0:no], in0=ps[:, gb, 0:no], scalar=cw0, in1=acc[ab][:, 0:no], op0=ALU.mult, op1=ALU.add),
                                   r=[PS(gb), "gv", ("f_acc", ab)], w=[("f_acc", ab)])
                                op("dve", lambda e, gb=gb, ab=ab, no=no, cw2=cw2: e.scalar_tensor_tensor(out=acc[ab][:, 0:no], in0=ps[:, gb, 2:2 + no], scalar=cw2, in1=acc[ab][:, 0:no], op0=ALU.mult, op1=ALU.add),
                                   r=[PS(gb), "gv", ("f_acc", ab)], w=[("f_acc", ab)])
                                op("act", lambda e, ab=ab, no=no: e.activation(out=sg[ab][:, 0:no], in_=acc[ab][:, 0:no], func=AF.Silu), r=[("f_acc", ab)], w=[("f_sg", ab)])
                                op("dve", lambda e, ab=ab, ub=ub, no=no, fi=fi: e.tensor_tensor(out=mT[:, fi, 0:no], in0=sg[ab][:, 0:no], in1=ps[:, ub, 0:no], op=ALU.mult), r=[("f_sg", ab), PS(ub)], w=["f_mT"])
                            for m in range(8):
                                ws = wc % 2
                                wc += 1
                                dma("fw%d" % ws, lambda q, ws=ws, m=m, half=half: q.dma_start(out=wdn[ws][:], in_=WDB[half, m]), r=[("WDB", half, m)], w=[("f_wdn", ws)])
                                bk = 6 + (m % 2)
                                for fi in range(NH):
                                    op("pe", lambda e, fi=fi, ws=ws, bk=bk, no=no: e.matmul(ps[:, bk, 0:no], wdn[ws][:, fi, :], mT[:, fi, 0:no], start=(fi == 0), stop=(fi == NH - 1)),
                                       r=[("f_wdn", ws), "f_mT"], w=[PS(bk)])
                                op("dve", lambda e, m=m, bk=bk, sl=sl, no=no: e.tensor_tensor(out=xb[sl][:, m, 1:1 + no], in0=xb[sl][:, m, 1:1 + no], in1=ps[:, bk, 0:no], op=ALU.add), r=[xk, PS(bk)], w=[xk])
                        dma("fs%d" % sl, lambda q, sl=sl, nm=nm, a0=a0, a1=a1, no=no: q.dma_start(out=X0[nm][:, :, a0:a1], in_=xb[sl][:, :, 1:1 + no]), r=[xk], w=xkeys("X0", nm, a0, a1))
                mk.barrier()

        with ExitStack() as st:
            tin = [sb("o_in%d" % i, [128, 8, 128], F32, st) for i in range(2)]
            tout = [sb("o_out%d" % i, [128, 1024], F32, st) for i in range(2)]
            it = 0
            for nm, S in seqs:
                for tt in range(S // 128):
                    sl = it % 2
                    it += 1
                    dma("oi%d" % sl, lambda q, sl=sl, tt=tt, nm=nm: q.dma_start(out=tin[sl][:], in_=X0[nm][:, :, tt * 128:(tt + 1) * 128]), r=[("X0", nm, tt)], w=[("o_in", sl)])
                    for c in range(8):
                        bk = (c // 4) + 2 * sl
                        op("pe", lambda e, c=c, bk=bk, sl=sl: e.transpose(ps[:, bk, (c % 4) * 128:(c % 4 + 1) * 128], tin[sl][:, c, :], ident[:]), r=[("o_in", sl), "ident"], w=[PS(bk)])
                    op("dve", lambda e, sl=sl: e.tensor_copy(out=tout[sl][:, 0:512], in_=ps[:, 2 * sl, :]), r=[PS(2 * sl)], w=[("o_out", sl)])
                    op("act", lambda e, sl=sl: e.activation(out=tout[sl][:, 512:1024], in_=ps[:, 2 * sl + 1, :], func=AF.Copy), r=[PS(2 * sl + 1)], w=[("o_out", sl)])
                    dma("oo%d" % sl, lambda q, sl=sl, tt=tt, nm=nm: q.dma_start(out=yout[nm][tt * 128:(tt + 1) * 128, :], in_=tout[sl][:]), r=[("o_out", sl)], w=[("Y", nm, tt)])
        mk.finish()
        nc._mk_nops = mk.nops
    return nc


_CFG = dict(SP=8192, SS=2048, NS=2, depth=4, ncores=8)


def kernel(**inputs):
    cfg = _CFG
    SP, SS, NS, depth, ncores = cfg["SP"], cfg["SS"], cfg["NS"], cfg["depth"], cfg["ncores"]
    inp = {k: np.asarray(v) for k, v in inputs.items()}
    nc = build_program(SP, SS, NS, depth)
    prep = _host_prep(inp, depth)
    consts = _const_tables(max(SP, SS))
    in_maps = []
    for c in range(ncores):
        m = {}
        if SP:
            m["x_p"] = np.ascontiguousarray(inp["x_prompt"][c])
            m["mem_p"] = np.ascontiguousarray(inp["mem_prompt"][c])
        for i in range(NS):
            m["x_s%d" % i] = np.ascontiguousarray(inp["x_sample"][c * NS + i])
            m["mem_s%d" % i] = np.ascontiguousarray(inp["mem_sample"][c * NS + i])
        m.update(prep)
        m.update(consts)
        in_maps.append(m)
    res = run_bass_kernel_spmd(nc, in_maps, core_ids=list(range(ncores)))
    outs = []
    if SP:
        outs.append(np.stack([res.results[c]["y_p"] for c in range(ncores)], axis=0).astype(np.float32))
    ys = np.stack([res.results[c]["y_s%d" % i] for c in range(ncores) for i in range(NS)], axis=0).astype(np.float32)
    outs.append(ys)
    return tuple(outs)
```
